# Optimizing a Trainium2 kernel written in Bass

```python
import math
import jax
import jax.numpy as jnp
from jax import lax
import numpy as np


D_MODEL = 2048
BATCH = 8
SEQ = 2048
DEPTH = 2

CTX_LEN = 256
GRID_W = 64
N_BRANCH = 3
BRANCH_W = D_MODEL // 2
ATT_HEADS = BRANCH_W // 256
QK_DIM = 128
V_DIM = 2 * QK_DIM
ATT_W = ATT_HEADS * V_DIM
SSM_GROUP = 16
SSM_GROUPS = BRANCH_W // SSM_GROUP
SSM_STATE = 64
DT_MIN = 1e-3
DT_MAX = 1e-1
CONV_K = 31
D_FF = (D_MODEL * 43) // 16
FFN_CONV_K = 3
Q_BLOCK = 128
ROPE_THETA = 10000.0
DEEPNORM_ALPHA = (2 * DEPTH) ** 0.25
DEEPNORM_BETA = (8 * DEPTH) ** -0.25
K_OFF = 0
V_OFF = K_OFF + ATT_W
U_OFF = V_OFF + ATT_W
Q_OFF = U_OFF + BRANCH_W
CONV_OFF = Q_OFF + ATT_W
GATE_OFF = CONV_OFF + 2 * BRANCH_W
IN_COLS = GATE_OFF + N_BRANCH * D_MODEL
CTX_COLS = Q_OFF

kernel_name = 'hybrid_diffattn_s5_conformer_convffn_block'


def layer_norm(x, gain=None, bias=None, eps=1e-6):
    xf = x.astype(jnp.float32)
    xc = xf - jnp.mean(xf, -1, keepdims=True)
    y = xc * lax.rsqrt(jnp.mean(xc * xc, -1, keepdims=True) + eps)
    if gain is not None:
        y = y * gain.astype(jnp.float32) + bias.astype(jnp.float32)
    return y.astype(x.dtype)


def depthwise_conv(x, w, b):
    k = w.shape[0]
    pad = (k - 1) // 2
    y = lax.conv_general_dilated(x, w[:, None, :].astype(x.dtype), (1,), [(pad, pad)],
                                 dimension_numbers=('NWC', 'WIO', 'NWC'),
                                 feature_group_count=x.shape[-1])
    return y + b


def axial_rope_tables(length, dtype):
    rows = length // GRID_W
    row = jnp.repeat(jnp.arange(rows, dtype=jnp.float32), GRID_W)
    col = jnp.tile(jnp.arange(GRID_W, dtype=jnp.float32), rows)
    n_freq = QK_DIM // 4
    inv = ROPE_THETA ** (-jnp.arange(n_freq, dtype=jnp.float32) / n_freq)
    ang = jnp.stack([row[:, None] * inv, col[:, None] * inv], axis=1)
    return jnp.cos(ang).astype(dtype), jnp.sin(ang).astype(dtype)


def apply_axial_rope(t, cos, sin):
    ts = t.reshape(t.shape[:-1] + (2, 2, QK_DIM // 4))
    t1 = ts[..., 0, :]
    t2 = ts[..., 1, :]
    c = cos[None, :, None, None]
    s = sin[None, :, None, None]
    out = jnp.stack([t1 * c - t2 * s, t2 * c + t1 * s], axis=-2)
    return out.reshape(t.shape)


def diff_softmax_mix(q, k, v, lam):
    s = jnp.einsum('bqhcd,bkhcd->bhcqk', q, k).astype(jnp.float32) * (QK_DIM ** -0.5)
    p = jax.nn.softmax(s, axis=-1)
    a = p[:, :, 0] - lam * p[:, :, 1]
    return jnp.einsum('bhqk,bkhe->bqhe', a.astype(v.dtype), v)


def diff_heads_out(o, gain, lam_init, w_o):
    of = o.astype(jnp.float32)
    of = of * lax.rsqrt(jnp.mean(of * of, -1, keepdims=True) + 1e-5) * gain.astype(jnp.float32) * (1.0 - lam_init)
    return of.astype(o.dtype).reshape(o.shape[0], o.shape[1], ATT_W) @ w_o


def s5_discretize(lam_re, lam_im, log_dt, b_re, b_im):
    lam = lax.complex(lam_re.astype(jnp.float32), lam_im.astype(jnp.float32))
    dt = jnp.exp(log_dt.astype(jnp.float32))[:, None]
    a_bar = jnp.exp(lam * dt)
    b = lax.complex(b_re.astype(jnp.float32), b_im.astype(jnp.float32))
    b_bar = ((a_bar - 1.0) / lam)[..., None] * b
    return a_bar, b_bar


def lti_scan(a_bar, bu, reverse):
    a = jnp.broadcast_to(a_bar, (bu.shape[0],) + a_bar.shape)

    def combine(e1, e2):
        a1, b1 = e1
        a2, b2 = e2
        return a1 * a2, a2[:, None] * b1 + b2

    return lax.associative_scan(combine, (a, bu), reverse=reverse)[1]


def s5_readout(s, cmat):
    n, bsz = s.shape[0], s.shape[1]
    return jnp.real(jnp.einsum('lbgp,ghp->blgh', s, cmat)).reshape(bsz, n, BRANCH_W)


def s5_branch(u_x, u_c, lam_re, lam_im, log_dt, b_re, b_im, c_re, c_im, d, w_glu, b_glu, w_o, with_ctx):
    bsz, seq, _ = u_x.shape
    n_ctx = u_c.shape[1]
    ux = u_x.astype(jnp.float32).reshape(bsz, seq, SSM_GROUPS, SSM_GROUP)
    uc = u_c.astype(jnp.float32).reshape(bsz, n_ctx, SSM_GROUPS, SSM_GROUP)
    d32 = d.astype(jnp.float32)
    y_x = d32 * u_x.astype(jnp.float32)
    y_c = d32 * u_c.astype(jnp.float32) if with_ctx else None
    for dirn in range(2):
        reverse = dirn == 1
        a_bar, b_bar = s5_discretize(lam_re[dirn], lam_im[dirn], log_dt[dirn], b_re[dirn], b_im[dirn])
        cmat = lax.complex(c_re[dirn].astype(jnp.float32), c_im[dirn].astype(jnp.float32))
        s_c = lti_scan(a_bar, jnp.einsum('blgh,gph->lbgp', uc, b_bar), reverse)
        h0 = s_c[0] if reverse else s_c[-1]
        bu_x = jnp.einsum('blgh,gph->lbgp', ux, b_bar)
        bu_x = bu_x.at[-1 if reverse else 0].add(a_bar * h0)
        s_x = lti_scan(a_bar, bu_x, reverse)
        y_x = y_x + s5_readout(s_x, cmat)
        if with_ctx:
            y_c = y_c + s5_readout(s_c, cmat)

    def glu_out(y):
        g = jax.nn.gelu(y.astype(u_x.dtype))
        return (g * jax.nn.sigmoid(g @ w_glu + b_glu)) @ w_o

    return glu_out(y_x), (glu_out(y_c) if with_ctx else None)


def conformer_conv_branch(z, conv_w, conv_b, ln_g, ln_b, w_o):
    zz = z[..., CONV_OFF:CONV_OFF + 2 * BRANCH_W]
    y = zz[..., :BRANCH_W] * jax.nn.sigmoid(zz[..., BRANCH_W:])
    y = depthwise_conv(y, conv_w, conv_b)
    y = jax.nn.silu(layer_norm(y, ln_g, ln_b, 1e-5))
    return y @ w_o


def gated_merge(z, o_att, o_ssm, o_conv, w_out):
    g = jax.nn.sigmoid(z[..., GATE_OFF:]).reshape(z.shape[:-1] + (N_BRANCH, D_MODEL))
    m = g[..., 0, :] * o_att + g[..., 1, :] * o_ssm + g[..., 2, :] * o_conv
    return m @ w_out


def token_mixer(hx, hc, w_in, att_lam, att_subln, w_att_o,
                ssm_lam_re, ssm_lam_im, ssm_log_dt, ssm_b_re, ssm_b_im, ssm_c_re, ssm_c_im,
                ssm_d, w_ssm_glu, b_ssm_glu, w_ssm_o,
                conv_w, conv_b, conv_ln_g, conv_ln_b, w_conv_o, w_out,
                lam_init, cos, sin, with_ctx):
    bsz, seq, _ = hx.shape
    n_ctx = hc.shape[1]
    zx = hx @ w_in
    zc = hc @ (w_in if with_ctx else w_in[:, :CTX_COLS])

    lf = att_lam.astype(jnp.float32)
    lam = jnp.exp(jnp.sum(lf[0] * lf[1])) - jnp.exp(jnp.sum(lf[2] * lf[3])) + lam_init

    def heads_qk(z, off, n):
        return z[..., off:off + ATT_W].reshape(bsz, n, ATT_HEADS, 2, QK_DIM)

    def heads_v(z, n):
        return z[..., V_OFF:V_OFF + ATT_W].reshape(bsz, n, ATT_HEADS, V_DIM)

    k_c = heads_qk(zc, K_OFF, n_ctx)
    v_c = heads_v(zc, n_ctx)
    k_x = apply_axial_rope(heads_qk(zx, K_OFF, seq), cos, sin)
    q_x = apply_axial_rope(heads_qk(zx, Q_OFF, seq), cos, sin)
    k_all = jnp.concatenate([k_c, k_x], axis=1)
    v_all = jnp.concatenate([v_c, heads_v(zx, seq)], axis=1)
    n_blk = seq // Q_BLOCK
    q_blocks = q_x.reshape(bsz, n_blk, Q_BLOCK, ATT_HEADS, 2, QK_DIM).transpose(1, 0, 2, 3, 4, 5)
    o_blocks = lax.map(lambda qb: diff_softmax_mix(qb, k_all, v_all, lam), q_blocks)
    o_x = o_blocks.transpose(1, 0, 2, 3, 4).reshape(bsz, seq, ATT_HEADS, V_DIM)
    att_x = diff_heads_out(o_x, att_subln, lam_init, w_att_o)

    ssm_x, ssm_c = s5_branch(zx[..., U_OFF:U_OFF + BRANCH_W], zc[..., U_OFF:U_OFF + BRANCH_W],
                             ssm_lam_re, ssm_lam_im, ssm_log_dt, ssm_b_re, ssm_b_im, ssm_c_re, ssm_c_im,
                             ssm_d, w_ssm_glu, b_ssm_glu, w_ssm_o, with_ctx)

    conv_x = conformer_conv_branch(zx, conv_w, conv_b, conv_ln_g, conv_ln_b, w_conv_o)

    out_x = gated_merge(zx, att_x, ssm_x, conv_x, w_out)
    if not with_ctx:
        return out_x, None
    o_c = diff_softmax_mix(heads_qk(zc, Q_OFF, n_ctx), k_c, v_c, lam)
    att_c = diff_heads_out(o_c, att_subln, lam_init, w_att_o)
    conv_c = conformer_conv_branch(zc, conv_w, conv_b, conv_ln_g, conv_ln_b, w_conv_o)
    out_c = gated_merge(zc, att_c, ssm_c, conv_c, w_out)
    return out_x, out_c


def conv_ffn(h, w_up, conv_w, conv_b, w_down):
    u = depthwise_conv(h @ w_up, conv_w, conv_b)
    return (jax.nn.silu(u[..., :D_FF]) * u[..., D_FF:]) @ w_down


def setup_inputs(seed: int = 0) -> dict:
    key = jax.random.key(seed)
    ks = iter(jax.random.split(key, 64))
    f32 = jnp.float32

    def nrm(shape, std):
        return std * jax.random.normal(next(ks), shape, f32)

    L = DEPTH
    G, P = SSM_GROUPS, SSM_STATE
    n_idx = jnp.arange(P, dtype=f32)
    return {
        'x': nrm((BATCH, SEQ, D_MODEL), 1.0),
        'c': nrm((BATCH, D_MODEL), 1.0),
        'ctx': nrm((BATCH, CTX_LEN, D_MODEL), 1.0),
        'c_ctx': nrm((D_MODEL,), 1.0),
        'w_ada': nrm((L, D_MODEL, 6 * D_MODEL), D_MODEL ** -0.5),
        'b_ada': nrm((L, 6 * D_MODEL), 0.01),
        'w_in': nrm((L, D_MODEL, IN_COLS), D_MODEL ** -0.5),
        'att_lam': nrm((L, 4, QK_DIM), 0.1),
        'att_subln': 1.0 + nrm((L, V_DIM), 0.02),
        'w_att_o': nrm((L, ATT_W, D_MODEL), ATT_W ** -0.5),
        'ssm_lam_re': -0.5 + nrm((L, 2, G, P), 0.01),
        'ssm_lam_im': math.pi * n_idx + nrm((L, 2, G, P), 0.01),
        'ssm_log_dt': jax.random.uniform(next(ks), (L, 2, G), f32, math.log(DT_MIN), math.log(DT_MAX)),
        'ssm_b_re': nrm((L, 2, G, P, SSM_GROUP), (2 * SSM_GROUP) ** -0.5),
        'ssm_b_im': nrm((L, 2, G, P, SSM_GROUP), (2 * SSM_GROUP) ** -0.5),
        'ssm_c_re': nrm((L, 2, G, SSM_GROUP, P), 0.5 ** 0.5),
        'ssm_c_im': nrm((L, 2, G, SSM_GROUP, P), 0.5 ** 0.5),
        'ssm_d': nrm((L, BRANCH_W), 1.0),
        'w_ssm_glu': nrm((L, BRANCH_W, BRANCH_W), BRANCH_W ** -0.5),
        'b_ssm_glu': nrm((L, BRANCH_W), 0.01),
        'w_ssm_o': nrm((L, BRANCH_W, D_MODEL), BRANCH_W ** -0.5),
        'conv_w': nrm((L, CONV_K, BRANCH_W), CONV_K ** -0.5),
        'conv_b': nrm((L, BRANCH_W), 0.01),
        'conv_ln_g': 1.0 + nrm((L, BRANCH_W), 0.02),
        'conv_ln_b': nrm((L, BRANCH_W), 0.01),
        'w_conv_o': nrm((L, BRANCH_W, D_MODEL), BRANCH_W ** -0.5),
        'w_out': nrm((L, D_MODEL, D_MODEL), DEEPNORM_BETA * D_MODEL ** -0.5),
        'ln1_g': 1.0 + nrm((L, D_MODEL), 0.02),
        'ln1_b': nrm((L, D_MODEL), 0.01),
        'w_up': nrm((L, D_MODEL, 2 * D_FF), D_MODEL ** -0.5),
        'ffn_conv_w': nrm((L, FFN_CONV_K, 2 * D_FF), FFN_CONV_K ** -0.5),
        'ffn_conv_b': nrm((L, 2 * D_FF), 0.01),
        'w_down': nrm((L, D_FF, D_MODEL), DEEPNORM_BETA * D_FF ** -0.5),
        'ln2_g': 1.0 + nrm((L, D_MODEL), 0.02),
        'ln2_b': nrm((L, D_MODEL), 0.01),
    }


def reference(x, c, ctx, c_ctx, w_ada, b_ada, w_in, att_lam, att_subln, w_att_o,
              ssm_lam_re, ssm_lam_im, ssm_log_dt, ssm_b_re, ssm_b_im, ssm_c_re, ssm_c_im,
              ssm_d, w_ssm_glu, b_ssm_glu, w_ssm_o,
              conv_w, conv_b, conv_ln_g, conv_ln_b, w_conv_o, w_out, ln1_g, ln1_b,
              w_up, ffn_conv_w, ffn_conv_b, w_down, ln2_g, ln2_b):
    bsz, seq, _ = x.shape
    cos, sin = axial_rope_tables(seq, x.dtype)
    silu_c = jax.nn.silu(c)
    silu_cc = jax.nn.silu(c_ctx)
    for l in range(DEPTH):
        last = l == DEPTH - 1
        lam_init = 0.8 - 0.6 * math.exp(-0.3 * l)
        mx = (silu_c @ w_ada[l] + b_ada[l]).reshape(bsz, 6, 1, D_MODEL)
        mc = (silu_cc @ w_ada[l] + b_ada[l]).reshape(6, D_MODEL)
        hx = layer_norm(x) * (1 + mx[:, 1]) + mx[:, 0]
        hc = layer_norm(ctx) * (1 + mc[1]) + mc[0]
        yx, yc = token_mixer(hx, hc, w_in[l], att_lam[l], att_subln[l], w_att_o[l],
                             ssm_lam_re[l], ssm_lam_im[l], ssm_log_dt[l], ssm_b_re[l], ssm_b_im[l],
                             ssm_c_re[l], ssm_c_im[l], ssm_d[l], w_ssm_glu[l], b_ssm_glu[l], w_ssm_o[l],
                             conv_w[l], conv_b[l], conv_ln_g[l], conv_ln_b[l], w_conv_o[l], w_out[l],
                             lam_init, cos, sin, not last)
        x = layer_norm(DEEPNORM_ALPHA * x + mx[:, 2] * yx, ln1_g[l], ln1_b[l], 1e-5)
        hx = layer_norm(x) * (1 + mx[:, 4]) + mx[:, 3]
        x = layer_norm(DEEPNORM_ALPHA * x + mx[:, 5] * conv_ffn(hx, w_up[l], ffn_conv_w[l], ffn_conv_b[l], w_down[l]),
                       ln2_g[l], ln2_b[l], 1e-5)
        if not last:
            ctx = layer_norm(DEEPNORM_ALPHA * ctx + mc[2] * yc, ln1_g[l], ln1_b[l], 1e-5)
            hc = layer_norm(ctx) * (1 + mc[4]) + mc[3]
            ctx = layer_norm(DEEPNORM_ALPHA * ctx + mc[5] * conv_ffn(hc, w_up[l], ffn_conv_w[l], ffn_conv_b[l], w_down[l]),
                             ln2_g[l], ln2_b[l], 1e-5)
    return x
```

```python
import contextlib
import math
import numpy as np
import concourse.bass as bass
import concourse.mybir as mybir
from concourse.bass_utils import run_bass_kernel_spmd

F32 = mybir.dt.float32
BF16 = mybir.dt.bfloat16
AF = mybir.ActivationFunctionType
ALU = mybir.AluOpType
AX = mybir.AxisListType

EPOCH = 1000000000

D = 2048
KT = 16
SEQ = 2048
CTXL = 256
T = SEQ + CTXL
NT = T // 128
DEPTH = 2
BW = 1024
DFF = 5504
FT = DFF // 128
INC = 12288
ALPHA = (2 * DEPTH) ** 0.25
CH = [(0, 256), (256, 512), (768, 512), (1280, 512), (1792, 512)]
TWO_PI = 2.0 * math.pi


class Prog:
    def __init__(self):
        self.nc = bass.Bass("TRN2", target_bir_lowering=False)
        nc = self.nc
        self.es = contextlib.ExitStack()
        self.eng = {"pe": nc.tensor, "act": nc.scalar, "dve": nc.vector,
                    "pool": nc.gpsimd, "sp": nc.sync}
        self.tick = {e: 0 for e in self.eng}
        self.psems = {e: [] for e in self.eng}
        self.seen = {e: {} for e in self.eng}
        self.res = {}
        self.dsem = {}
        self.nsem = 0
        self.ninstr = 0
        self.scopes = []
        self.kinds = {}
        self.free_dsems = []
        self.uid = 0
        self.pool_out = []
        self.log = None

    def sem(self, name):
        self.nsem += 1
        return self.es.enter_context(self.nc.semaphore(name))

    def push(self):
        self.scopes.append(contextlib.ExitStack())

    def pop(self):
        self.barrier()
        self.scopes.pop().close()

    def sbuf(self, name, shape, dt):
        st = self.scopes[-1] if self.scopes else self.es
        self.uid += 1
        return st.enter_context(self.nc.sbuf_tensor(f"{name}_{self.uid}", list(shape), dt))

    def psum(self, name, shape, dt=F32):
        st = self.scopes[-1] if self.scopes else self.es
        self.uid += 1
        return st.enter_context(self.nc.psum_tensor(f"{name}_{self.uid}", list(shape), dt))

    def dram(self, name, shape, dt, kind=None):
        kind = kind or self.kinds.get(name, "Internal")
        return self.nc.dram_tensor(name, list(shape), dt, kind=kind).ap()

    def _r(self, key):
        r = self.res.get(key)
        if r is None:
            r = self.res[key] = {"w": {}, "r": {}}
        return r

    def _emit_waits(self, e, need):
        seen = self.seen[e]
        for s, (h, v, src) in need.items():
            if seen.get(s, 0) >= v:
                continue
            self.eng[e].wait_ge(h, v)
            if self.log is not None:
                self.log[e].append(("w", id(h), v))
            self.ninstr += 1
            seen[s] = v

    @staticmethod
    def _isd(k):
        return isinstance(k, str) and k.startswith("D:")

    def _waits(self, e, reads, writes, is_dma=False):
        need = {}
        for k in reads:
            for s, v in self._r(k)["w"].items():
                if e == "pe" and v[2] == "pe" and not is_dma:
                    continue
                if need.get(s, (None, 0))[1] < v[1]:
                    need[s] = v
        for k in writes:
            r = self._r(k)
            for d in ((r["r"],) if self._isd(k) else (r["w"], r["r"])):
                for s, v in d.items():
                    if v[2] == e and not is_dma:
                        continue
                    if need.get(s, (None, 0))[1] < v[1]:
                        need[s] = v
        self._emit_waits(e, need)

    def _commit(self, ev, reads, writes):
        s = ev[0]
        for k in reads:
            r = self._r(k)["r"]
            if r.get(s, (None, 0, None))[1] < ev[1][1]:
                r[s] = ev[1]
        for k in writes:
            r = self._r(k)
            if self._isd(k):
                if r["w"].get(s, (None, 0, None))[1] < ev[1][1]:
                    r["w"][s] = ev[1]
            else:
                r["w"] = {s: ev[1]}
                r["r"] = {}

    def op(self, e, fn, reads=(), writes=()):
        self._waits(e, reads, writes)
        ins = fn()
        self.ninstr += 1
        t = self.tick[e]
        ep = t // EPOCH
        while len(self.psems[e]) <= ep:
            self.psems[e].append(self.sem(f"p_{e}_{len(self.psems[e])}"))
        h = self.psems[e][ep]
        ins.then_inc(h, 1)
        if self.log is not None:
            self.log[e].append(("i", id(h), 1))
        self.tick[e] = t + 1
        self._commit(((e, ep), (h, t % EPOCH + 1, e)), reads, writes)
        return ins

    def dma(self, q, out, in_, reads=(), writes=(), key=None, **kw):
        assert key is not None
        self._waits(q, reads, writes, is_dma=True)
        ent = self.dsem.get(key)
        if ent is None or ent[1] * 16 >= 11000:
            ent = None
            while self.free_dsems:
                c = self.free_dsems.pop()
                if c[1] * 16 < 8000:
                    self.uid += 1
                    ent = [c[0], c[1], self.uid]
                    break
            if ent is None:
                self.uid += 1
                ent = [self.sem(f"d{self.uid}"), 0, self.uid]
            self.dsem[key] = ent
        if q == "pool":
            while len(self.pool_out) >= 3:
                sk, ev = self.pool_out.pop(0)
                self._emit_waits("pool", {sk: ev})
        ins = self.eng[q].dma_start(out=out, in_=in_, **kw)
        self.ninstr += 1
        ent[1] += 1
        ins.then_inc(ent[0], 16)
        if self.log is not None:
            self.log[q].append(("i", id(ent[0]), 16))
        skey = ("dma", key, ent[2])
        ev = (ent[0], ent[1] * 16, "dma")
        if q == "pool":
            self.pool_out.append((skey, ev))
        self._commit((skey, ev), reads, writes)
        return ins

    def barrier(self):
        need = {}
        for r in self.res.values():
            for d in (r["w"], r["r"]):
                for s, v in d.items():
                    if need.get(s, (None, 0))[1] < v[1]:
                        need[s] = v
        for e in self.eng:
            self._emit_waits(e, dict(need))
        self.res = {}
        self.free_dsems.extend([v[0], v[1]] for v in self.dsem.values())
        self.dsem = {}

    def close(self):
        self.es.close()


def bcast_rows(ap, parts=128):
    n = ap.shape[-1]
    return bass.AP(ap.tensor, ap.offset, [[0, parts], [1, n]])


def rev_free(ap2d):
    a = ap2d.ap
    assert len(a) == 2
    return bass.AP(ap2d.tensor, ap2d.offset + a[1][0] * (a[1][1] - 1), [list(a[0]), [-a[1][0], a[1][1]]])


C_IDENT = 0
C_PERM = 128
C_IOTA = 256
C_ROPEC = C_IOTA + T
C_ROPES = C_ROPEC + SEQ
C_CMASK = C_ROPES + SEQ
C_ONES = C_CMASK + 8
C_END = C_ONES + 128


def host_consts():
    c = np.zeros((128, C_END), np.float32)
    c[:, C_IDENT:C_IDENT + 128] = np.eye(128, dtype=np.float32)
    d = np.arange(128)
    partner = np.where((d % 64) < 32, d + 32, d - 32)
    c[partner, C_PERM + d] = 1.0
    c[:, C_IOTA:C_IOTA + T] = np.arange(T, dtype=np.float32)[None, :]
    t = np.arange(SEQ)
    row = (t // 64).astype(np.float32)
    col = (t % 64).astype(np.float32)
    f = (d % 32).astype(np.float32)
    inv = (np.float32(10000.0) ** (-f / np.float32(32.0))).astype(np.float32)
    pos = np.where(((d // 64) == 0)[:, None], row[None, :], col[None, :]).astype(np.float32)
    ang = (pos * inv[:, None]).astype(np.float32)
    c[:, C_ROPEC:C_ROPEC + SEQ] = np.cos(ang)
    sgn = np.where((d % 64) < 32, -1.0, 1.0).astype(np.float32)
    c[:, C_ROPES:C_ROPES + SEQ] = np.sin(ang) * sgn[:, None]
    for jj in range(4):
        for gl in range(2):
            c[:, C_CMASK + jj * 2 + gl] = ((d // 32) == jj) & (((d // 16) % 2) == gl)
    c[:, C_ONES:C_ONES + 128] = 1.0
    return c


class G:
    pass


WNAMES = ["w_ada", "b_ada", "w_in", "att_lam", "att_subln", "w_att_o",
          "ssm_lam_re", "ssm_lam_im", "ssm_log_dt", "ssm_b_re", "ssm_b_im", "ssm_c_re", "ssm_c_im",
          "ssm_d", "w_ssm_glu", "b_ssm_glu", "w_ssm_o", "conv_w", "conv_b", "conv_ln_g", "conv_ln_b",
          "w_conv_o", "w_out", "ln1_g", "ln1_b", "w_up", "ffn_conv_w", "ffn_conv_b", "w_down",
          "ln2_g", "ln2_b"]
WSHAPES = {
    "w_ada": [2, D, 6 * D], "b_ada": [2, 6 * D], "w_in": [2, D, INC], "att_lam": [2, 4, 128],
    "att_subln": [2, 256], "w_att_o": [2, BW, D], "ssm_lam_re": [2, 2, 64, 64], "ssm_lam_im": [2, 2, 64, 64],
    "ssm_log_dt": [2, 2, 64], "ssm_b_re": [2, 2, 64, 64, 16], "ssm_b_im": [2, 2, 64, 64, 16],
    "ssm_c_re": [2, 2, 64, 16, 64], "ssm_c_im": [2, 2, 64, 16, 64], "ssm_d": [2, BW],
    "w_ssm_glu": [2, BW, BW], "b_ssm_glu": [2, BW], "w_ssm_o": [2, BW, D], "conv_w": [2, 31, BW],
    "conv_b": [2, BW], "conv_ln_g": [2, BW], "conv_ln_b": [2, BW], "w_conv_o": [2, BW, D],
    "w_out": [2, D, D], "ln1_g": [2, D], "ln1_b": [2, D], "w_up": [2, D, 2 * DFF],
    "ffn_conv_w": [2, 3, 2 * DFF], "ffn_conv_b": [2, 2 * DFF], "w_down": [2, DFF, D],
    "ln2_g": [2, D], "ln2_b": [2, D],
}
SCRATCH = {
    "mod": ([DEPTH, 2, 6 * D], F32),
    "kT": ([BW, T], BF16), "qT": ([BW, T], BF16), "v": ([T, BW], BF16),
    "uT": ([BW, T], F32), "cab": ([2 * BW, T], F32), "gate": ([3 * D, T], F32),
    "attnT": ([BW, T], BF16), "ssmT": ([BW, T], BF16), "ycnT": ([BW, T], BF16),
    "yfb": ([2, BW, T], F32), "convo": ([BW, T], F32),
    "S1": ([T, D], F32), "S2": ([T, D], F32), "aT": ([DFF, T], BF16),
    "wao_b": ([16, 128, 8, 128], BF16), "wso_b": ([16, 128, 8, 128], BF16), "wco_b": ([16, 128, 8, 128], BF16),
    "wout_b": ([4, 128, 16, 512], BF16), "wdn_b": ([8, 128, FT, 256], BF16),
}


def declare(P, ext_in=(), ext_out=(), wnames=None):
    g = G()
    g.x = P.dram("x", [SEQ, D], F32, "ExternalInput")
    g.ctx = P.dram("ctx", [CTXL, D], F32, "ExternalInput")
    g.cvec = P.dram("cvec", [128, KT, 2], F32, "ExternalInput")
    g.cst = P.dram("cst", [128, C_END], F32, "ExternalInput")
    for n in (WNAMES if wnames is None else wnames):
        setattr(g, n, P.dram(n, WSHAPES[n], F32, "ExternalInput"))
    g.out = P.dram("out", [SEQ, D], F32, "ExternalOutput")
    for n, (shp, dt) in SCRATCH.items():
        kind = "ExternalInput" if n in ext_in else ("ExternalOutput" if n in ext_out else "Internal")
        setattr(g, n, P.dram(n, shp, dt, kind))
    return g


def load_consts(P, g):
    nc = P.nc
    k = G()
    k.ident = P.sbuf("k_ident", [128, 128], F32)
    k.identb = P.sbuf("k_identb", [128, 128], BF16)
    k.ones = P.sbuf("k_ones", [128, 128], F32)
    k.onesb = P.sbuf("k_onesb", [128, 128], BF16)
    P.dma("sp", k.ident[:], g.cst[:, C_IDENT:C_IDENT + 128], writes=["k_ident"], key=("k_ident", "ld"))
    P.dma("sp", k.ones[:], g.cst[:, C_ONES:C_ONES + 128], writes=["k_ones"], key=("k_ones", "ld"))
    P.op("dve", lambda: nc.vector.tensor_copy(k.identb[:], k.ident[:]), reads=["k_ident"], writes=["k_identb"])
    P.op("dve", lambda: nc.vector.tensor_copy(k.onesb[:], k.ones[:]), reads=["k_ones"], writes=["k_onesb"])
    return k


def phase_ada(P, g, k, l):
    nc = P.nc
    P.push()
    craw = P.sbuf("a_craw", [128, KT, 2], F32)
    sc = P.sbuf("a_sc", [128, KT, 2], F32)
    slab = [P.sbuf(f"a_slab{i}", [128, KT, 512], F32) for i in range(2)]
    bt = [P.sbuf(f"a_bt{i}", [2, 512], F32) for i in range(2)]
    orow = [P.sbuf(f"a_or{i}", [2, 512], F32) for i in range(2)]
    ps = [P.psum(f"a_ps{i}", [128, 512]) for i in range(2)]
    P.dma("sp", craw[:], g.cvec, writes=["a_craw"], key=("a_craw", "ld"))
    P.op("act", lambda: nc.scalar.activation(sc[:], craw[:], AF.Silu), reads=["a_craw"], writes=["a_sc"])
    for cc in range(24):
        i = cc % 2
        cols = slice(cc * 512, (cc + 1) * 512)
        P.dma("sp", slab[i][:], g.w_ada[l, :, cols].rearrange("(kt p) n -> p kt n", p=128),
              writes=[f"a_slab{i}"], key=(f"a_slab{i}", "ld"))
        P.dma("sp", bt[i][:], bcast_rows(g.b_ada[l, cols], 2), writes=[f"a_bt{i}"], key=(f"a_bt{i}", "ld"))
        for kt in range(KT):
            P.op("pe", lambda kt=kt: nc.tensor.matmul(ps[i][0:2, :], lhsT=sc[:, kt, :], rhs=slab[i][:, kt, :],
                                                      start=(kt == 0), stop=(kt == KT - 1)),
                 reads=["a_sc", f"a_slab{i}"], writes=[f"a_ps{i}"])
        P.op("dve", lambda: nc.vector.tensor_tensor(orow[i][:], ps[i][0:2, :], bt[i][:], ALU.add),
             reads=[f"a_ps{i}", f"a_bt{i}"], writes=[f"a_or{i}"])
        P.dma("sp", g.mod[l, :, cols], orow[i][:], reads=[f"a_or{i}"], writes=["D:mod"], key=(f"a_or{i}", "st"))
    P.pop()


def alloc_lnmod(P, pfx):
    w = G()
    w.pfx = pfx
    w.st = P.sbuf(pfx + "st", [128, 4, 6], F32)
    w.mv = P.sbuf(pfx + "mv", [128, 2], F32)
    w.rstd = P.sbuf(pfx + "rstd", [128, 1], F32)
    w.nmr = P.sbuf(pfx + "nmr", [128, 1], F32)
    w.xn = P.sbuf(pfx + "xn", [128, D], F32)
    w.hb = [P.sbuf(pfx + f"hb{i}", [128, D], BF16) for i in range(2)]
    w.pT = [P.psum(pfx + f"pT{i}", [128, 1024], BF16) for i in range(2)]
    w.n = 0
    return w


def rsqrt_col(P, out, okey, in_, ikey, eps, mul):
    nc = P.nc
    P.op("dve", lambda: nc.vector.tensor_scalar(out, in_, mul, eps, ALU.mult, ALU.add), reads=[ikey], writes=[okey])
    P.op("act", lambda: nc.scalar.sqrt(out, out), reads=[okey], writes=[okey])
    P.op("dve", lambda: nc.vector.reciprocal(out, out), reads=[okey], writes=[okey])


def ln_stats(P, w, xt, xkey, eps):
    nc = P.nc
    p = w.pfx
    for i in range(4):
        P.op("dve", lambda i=i: nc.vector.bn_stats(w.st[:, i, :], xt[:, i * 512:(i + 1) * 512]),
             reads=[xkey], writes=[p + "st"])
    P.op("dve", lambda: nc.vector.bn_aggr(w.mv[:], w.st[:]), reads=[p + "st"], writes=[p + "mv"])
    rsqrt_col(P, w.rstd[:], p + "rstd", w.mv[:, 1:2], p + "mv", eps, 1.0)
    P.op("dve", lambda: nc.vector.tensor_scalar(w.nmr[:], w.mv[:, 0:1], w.rstd[:], -1.0, ALU.mult, ALU.mult),
         reads=[p + "mv", p + "rstd"], writes=[p + "nmr"])


def ln_mod_T(P, k, w, xt, xkey, scb, sckey, shb, shkey, hT, hkey, ti):
    nc = P.nc
    p = w.pfx
    ln_stats(P, w, xt, xkey, 1e-6)
    P.op("act", lambda: nc.scalar.activation(w.xn[:], xt[:], AF.Identity, bias=w.nmr[:], scale=w.rstd[:]),
         reads=[xkey, p + "nmr", p + "rstd"], writes=[p + "xn"])
    i = w.n % 2
    w.n += 1
    P.op("dve", lambda: nc.vector.tensor_tensor(w.xn[:], w.xn[:], scb[:], ALU.mult),
         reads=[p + "xn", sckey], writes=[p + "xn"])
    P.op("dve", lambda: nc.vector.tensor_tensor(w.hb[i][:], w.xn[:], shb[:], ALU.add),
         reads=[p + "xn", shkey], writes=[p + f"hb{i}"])
    for half in range(2):
        for j in range(8):
            kt = half * 8 + j
            P.op("pe", lambda kt=kt, j=j: nc.tensor.transpose(w.pT[half][:, j * 128:(j + 1) * 128],
                                                               w.hb[i][:, kt * 128:(kt + 1) * 128], k.identb[:]),
                 reads=[p + f"hb{i}", "k_identb"], writes=[p + f"pT{half}"])
        dst = hT[:, half * 8:(half + 1) * 8, ti * 128:(ti + 1) * 128]
        src = w.pT[half][:].rearrange("p (a b) -> p a b", b=128)
        if half == 0:
            P.op("act", lambda: nc.scalar.copy(dst, src), reads=[p + f"pT{half}"], writes=[(hkey, ti)])
        else:
            P.op("dve", lambda: nc.vector.tensor_copy(dst, src), reads=[p + f"pT{half}"], writes=[(hkey, ti)])


def load_mod_bcast(P, g, l, r, idx, tile, key, plus_one=False):
    nc = P.nc
    P.dma("sp", tile[:], bcast_rows(g.mod[l, r, idx * D:(idx + 1) * D]), reads=["D:mod"], writes=[key],
          key=(key, "ld"))
    if plus_one:
        P.op("pool", lambda: nc.gpsimd.tensor_scalar_add(tile[:], tile[:], 1.0), reads=[key], writes=[key])


def stream_rows(g, src, ti):
    if src == "in":
        if ti < 2:
            return g.ctx[ti * 128:(ti + 1) * 128, :], "D:in"
        return g.x[(ti - 2) * 128:(ti - 1) * 128, :], "D:in"
    return getattr(g, src)[ti * 128:(ti + 1) * 128, :], "D:" + src


def phase_inproj(P, g, k, l, src):
    nc = P.nc
    P.push()
    hT = P.sbuf("hT", [128, KT, T], BF16)
    P.push()
    w = alloc_lnmod(P, "b_")
    scb = [P.sbuf(f"b_scb{r}", [128, D], F32) for r in range(2)]
    shb = [P.sbuf(f"b_shb{r}", [128, D], F32) for r in range(2)]
    for r in range(2):
        load_mod_bcast(P, g, l, r, 1, scb[r], f"b_scb{r}", plus_one=True)
        load_mod_bcast(P, g, l, r, 0, shb[r], f"b_shb{r}")
    xt = [P.sbuf(f"b_xt{i}", [128, D], F32) for i in range(2)]
    for ti in range(NT):
        i = ti % 2
        r = 1 if ti < 2 else 0
        rows, dk = stream_rows(g, src, ti)
        P.dma("sp", xt[i][:], rows, reads=[dk], writes=[f"b_xt{i}"], key=(f"b_xt{i}", "ld"))
        ln_mod_T(P, k, w, xt[i], f"b_xt{i}", scb[r], f"b_scb{r}", shb[r], f"b_shb{r}", hT, "hT", ti)
    P.pop()
    hkeys = [("hT", ti) for ti in range(NT)]

    P.push()
    ws = [P.sbuf(f"c_ws{i}", [128, KT, 512], BF16) for i in range(2)]
    ps = [P.psum(f"c_ps{i}", [128, 512]) for i in range(4)]
    psw = [P.psum(f"c_psw{i}", [128, 512]) for i in range(2)]
    stg = [P.sbuf(f"c_stg{i}", [128, 512], F32) for i in range(4)]
    stb = [P.sbuf(f"c_stb{i}", [128, 512], BF16) for i in range(4)]
    tmp = [P.sbuf(f"c_tmp{i}", [128, 512], F32) for i in range(2)]
    r1 = [P.sbuf(f"c_r1{i}", [128, 512], F32) for i in range(2)]
    perm = P.sbuf("c_perm", [128, 128], F32)
    ropec = P.sbuf("c_ropec", [128, SEQ], F32)
    ropes = P.sbuf("c_ropes", [128, SEQ], F32)
    P.dma("sp", perm[:], g.cst[:, C_PERM:C_PERM + 128], writes=["c_perm"], key=("c_perm", "ld"))
    P.dma("sp", ropec[:], g.cst[:, C_ROPEC:C_ROPEC + SEQ], writes=["c_ropec"], key=("c_ropec", "ld"))
    P.dma("sp", ropes[:], g.cst[:, C_ROPES:C_ROPES + SEQ], writes=["c_ropes"], key=("c_ropes", "ld"))
    cnt = {"ps": 0, "stg": 0, "stb": 0, "rp": 0}

    def store(q, dst, tile, key, dkey):
        P.dma(q, dst, tile, reads=[key], writes=[dkey], key=(key, "st"))

    for sl in range(24):
        wi = sl % 2
        P.dma("pool", ws[wi][:], g.w_in[l, :, sl * 512:(sl + 1) * 512].rearrange("(kt p) n -> p kt n", p=128),
              writes=[f"c_ws{wi}"], key=(f"c_ws{wi}", "ld"))
        wkey = f"c_ws{wi}"
        if sl in (2, 3):
            for ti in range(NT):
                pi = cnt["ps"] % 4
                cnt["ps"] += 1
                for kt in range(KT):
                    P.op("pe", lambda kt=kt, ti=ti, pi=pi: nc.tensor.matmul(
                        ps[pi][:, :], lhsT=hT[:, kt, ti * 128:(ti + 1) * 128], rhs=ws[wi][:, kt, :],
                        start=(kt == 0), stop=(kt == KT - 1)),
                        reads=[("hT", ti), wkey], writes=[f"c_ps{pi}"])
                bi = cnt["stb"] % 4
                cnt["stb"] += 1
                P.op("act", lambda pi=pi, bi=bi: nc.scalar.copy(stb[bi][:], ps[pi][:]),
                     reads=[f"c_ps{pi}"], writes=[f"c_stb{bi}"])
                store("sp", g.v[ti * 128:(ti + 1) * 128, (sl - 2) * 512:(sl - 1) * 512], stb[bi][:], f"c_stb{bi}", "D:v")
            continue
        for cc in range(4):
            ct = sl * 4 + cc
            for (t0, tl) in CH:
                pi = cnt["ps"] % 4
                cnt["ps"] += 1
                tis = list(range(t0 // 128, (t0 + tl) // 128))
                for kt in range(KT):
                    P.op("pe", lambda kt=kt, pi=pi, t0=t0, tl=tl, cc=cc: nc.tensor.matmul(
                        ps[pi][:, :tl], lhsT=ws[wi][:, kt, cc * 128:(cc + 1) * 128], rhs=hT[:, kt, t0:t0 + tl],
                        start=(kt == 0), stop=(kt == KT - 1)),
                        reads=[("hT", ti) for ti in tis] + [wkey], writes=[f"c_ps{pi}"])
                pkey = f"c_ps{pi}"
                if ct < 8 or 24 <= ct < 32:
                    dst_t = g.kT if ct < 8 else g.qT
                    dk = "D:kT" if ct < 8 else "D:qT"
                    row0 = (ct % 8) * 128
                    bi = cnt["stb"] % 4
                    cnt["stb"] += 1
                    if t0 == 0:
                        P.op("act", lambda pi=pi, bi=bi, tl=tl: nc.scalar.copy(stb[bi][:, :tl], ps[pi][:, :tl]),
                             reads=[pkey], writes=[f"c_stb{bi}"])
                    else:
                        ri = cnt["rp"] % 2
                        cnt["rp"] += 1
                        p0 = t0 - CTXL
                        P.op("act", lambda pi=pi, ri=ri: nc.scalar.copy(tmp[ri][:], ps[pi][:]),
                             reads=[pkey], writes=[f"c_tmp{ri}"])
                        P.op("pe", lambda ri=ri: nc.tensor.matmul(psw[ri][:], lhsT=perm[:], rhs=tmp[ri][:],
                                                                  start=True, stop=True),
                             reads=["c_perm", f"c_tmp{ri}"], writes=[f"c_psw{ri}"])
                        P.op("dve", lambda ri=ri, p0=p0: nc.vector.tensor_tensor(
                            r1[ri][:], tmp[ri][:], ropec[:, p0:p0 + 512], ALU.mult),
                            reads=[f"c_tmp{ri}", "c_ropec"], writes=[f"c_r1{ri}"])
                        P.op("dve", lambda ri=ri, p0=p0: nc.vector.tensor_tensor(
                            tmp[ri][:], psw[ri][:], ropes[:, p0:p0 + 512], ALU.mult),
                            reads=[f"c_psw{ri}", "c_ropes"], writes=[f"c_tmp{ri}"])
                        P.op("dve", lambda ri=ri, bi=bi: nc.vector.tensor_tensor(
                            stb[bi][:], r1[ri][:], tmp[ri][:], ALU.add),
                            reads=[f"c_r1{ri}", f"c_tmp{ri}"], writes=[f"c_stb{bi}"])
                    store("sp", dst_t[row0:row0 + 128, t0:t0 + tl], stb[bi][:, :tl], f"c_stb{bi}", dk)
                else:
                    si = cnt["stg"] % 4
                    cnt["stg"] += 1
                    if 16 <= ct < 24:
                        dst, dk, fn = g.uT[(ct - 16) * 128:(ct - 15) * 128, t0:t0 + tl], "D:uT", AF.Copy
                    elif 32 <= ct < 40:
                        dst, dk, fn = g.cab[(ct - 32) * 128:(ct - 31) * 128, t0:t0 + tl], "D:cab", AF.Copy
                    elif 40 <= ct < 48:
                        dst, dk, fn = g.cab[(ct - 32) * 128:(ct - 31) * 128, t0:t0 + tl], "D:cab", AF.Sigmoid
                    else:
                        dst, dk, fn = g.gate[(ct - 48) * 128:(ct - 47) * 128, t0:t0 + tl], "D:gate", AF.Sigmoid
                    P.op("act", lambda pi=pi, si=si, tl=tl, fn=fn: nc.scalar.activation(
                        stg[si][:, :tl], ps[pi][:, :tl], fn), reads=[pkey], writes=[f"c_stg{si}"])
                    store("sp", dst, stg[si][:, :tl], f"c_stg{si}", dk)
    P.pop()
    P.pop()


def make_cvec(c_b, c_ctx):
    a = np.stack([np.asarray(c_b, np.float32).reshape(KT, 128).T,
                  np.asarray(c_ctx, np.float32).reshape(KT, 128).T], axis=-1)
    return np.ascontiguousarray(a)


def phase_attn(P, g, k, l):
    nc = P.nc
    lam_init = 0.8 - 0.6 * math.exp(-0.3 * l)
    scale = 128 ** -0.5
    P.push()
    lamb = P.sbuf("d_lamb", [128, 512], F32)
    lp = P.sbuf("d_lp", [128, 256], F32)
    ls = P.sbuf("d_ls", [128, 2], F32)
    nlam = P.sbuf("d_nlam", [128, 1], F32)
    gainb = P.sbuf("d_gainb", [128, 256], F32)
    P.dma("sp", lamb[:], bcast_rows(g.att_lam[l].rearrange("a b -> (a b)")), writes=["d_lamb"], key=("d_lamb", "ld"))
    P.dma("sp", gainb[:], bcast_rows(g.att_subln[l]), writes=["d_gainb"], key=("d_gainb", "ld"))
    P.op("dve", lambda: nc.vector.tensor_tensor(lp[:, 0:128], lamb[:, 0:128], lamb[:, 128:256], ALU.mult),
         reads=["d_lamb"], writes=["d_lp"])
    P.op("dve", lambda: nc.vector.tensor_tensor(lp[:, 128:256], lamb[:, 256:384], lamb[:, 384:512], ALU.mult),
         reads=["d_lamb"], writes=["d_lp"])
    P.op("dve", lambda: nc.vector.reduce_sum(ls[:], lp[:].rearrange("p (a b) -> p a b", b=128), axis=AX.X),
         reads=["d_lp"], writes=["d_ls"])
    P.op("act", lambda: nc.scalar.activation(ls[:], ls[:], AF.Exp), reads=["d_ls"], writes=["d_ls"])
    P.op("dve", lambda: nc.vector.tensor_tensor(nlam[:], ls[:, 1:2], ls[:, 0:1], ALU.subtract),
         reads=["d_ls"], writes=["d_nlam"])
    P.op("dve", lambda: nc.vector.tensor_scalar_add(nlam[:], nlam[:], -lam_init), reads=["d_nlam"], writes=["d_nlam"])
    P.op("dve", lambda: nc.vector.tensor_scalar_mul(gainb[:], gainb[:], 1.0 - lam_init), reads=["d_gainb"], writes=["d_gainb"])

    kTt = [P.sbuf(f"d_kT{i}", [128, 2, T], BF16) for i in range(2)]
    qTt = [P.sbuf(f"d_qT{i}", [128, 2, T], BF16) for i in range(2)]
    vau = [P.sbuf(f"d_va{i}", [128, NT, 257], BF16) for i in range(2)]
    pT = [P.sbuf(f"d_pT{i}", [128, 256], BF16) for i in range(4)]
    ps_s = [P.psum(f"d_pss{i}", [128, 512]) for i in range(3)]
    ps_o = [[P.psum(f"d_pso{c}{s}", [128, 512]) for s in range(2)] for c in range(2)]
    ps_t = P.psum("d_pst", [128, 256], BF16)
    rec = P.sbuf("d_rec", [128, 2], F32)
    t1 = P.sbuf("d_t1", [128, 256], F32)
    o = P.sbuf("d_o", [128, 256], F32)
    sq = P.sbuf("d_sq", [128, 256], F32)
    ss = P.sbuf("d_ss", [128, 1], F32)
    on = [P.sbuf(f"d_on{i}", [128, 256], BF16) for i in range(2)]
    ast = [P.sbuf(f"d_ast{i}", [128, 2, 128], BF16) for i in range(2)]
    osb = [[P.sbuf(f"d_osb{c}{s_}", [128, 257], F32) for s_ in range(2)] for c in range(2)]
    cnt = {"n": 0}
    qchunks = [(0, [0, 1])] + [(256 + 256 * i, list(range(NT))) for i in range(8)]
    LOOK = 2
    for hd in range(4):
        hi = hd % 2
        rows = slice(hd * 256, (hd + 1) * 256)
        P.dma("sp", kTt[hi][:], g.kT[rows, :].rearrange("(c p) t -> p c t", p=128), reads=["D:kT"],
              writes=[f"d_kT{hi}"], key=(f"d_kT{hi}", "ld"))
        P.dma("sp", qTt[hi][:], g.qT[rows, :].rearrange("(c p) t -> p c t", p=128), reads=["D:qT"],
              writes=[f"d_qT{hi}"], key=(f"d_qT{hi}", "ld"))
        P.dma("sp", vau[hi][:, :, 0:256], g.v[:, rows].rearrange("(n p) e -> p n e", p=128), reads=["D:v"],
              writes=[f"d_va{hi}"], key=(f"d_va{hi}", "ld"))
        P.op("pool", lambda hi=hi: nc.gpsimd.memset(vau[hi][:, :, 256:257], 1.0), reads=[], writes=[f"d_va{hi}o"])
        steps = []
        for (q0, kts) in qchunks:
            for c in range(2):
                for kt in kts:
                    steps.append((q0, kts, c, kt))

        def score(i):
            q0, kts, c, kt = steps[i]
            si = i % 3
            P.op("pe", lambda: nc.tensor.matmul(
                ps_s[si][:, 0:256], lhsT=kTt[hi][:, c, kt * 128:(kt + 1) * 128],
                rhs=qTt[hi][:, c, q0:q0 + 256], start=True, stop=True),
                reads=[f"d_kT{hi}", f"d_qT{hi}"], writes=[f"d_pss{si}"])

        def epilogue(q0):
            for c in range(2):
                for s_ in range(2):
                    P.op("act", lambda c=c, s_=s_: nc.scalar.copy(osb[c][s_][:], ps_o[c][s_][:, 0:257]),
                         reads=[f"d_pso{c}{s_}"], writes=[f"d_osb{c}{s_}"])
            for s_ in range(2):
                ni = cnt["n"] % 2
                cnt["n"] += 1
                P.op("dve", lambda s_=s_: nc.vector.reciprocal(rec[:, 0:1], osb[0][s_][:, 256:257]),
                     reads=[f"d_osb0{s_}"], writes=["d_rec0"])
                P.op("dve", lambda s_=s_: nc.vector.reciprocal(rec[:, 1:2], osb[1][s_][:, 256:257]),
                     reads=[f"d_osb1{s_}"], writes=["d_rec1"])
                P.op("dve", lambda: nc.vector.tensor_tensor(rec[:, 1:2], rec[:, 1:2], nlam[:], ALU.mult),
                     reads=["d_rec1", "d_nlam"], writes=["d_rec1"])
                P.op("dve", lambda s_=s_: nc.vector.tensor_scalar_mul(t1[:], osb[0][s_][:, 0:256], rec[:, 0:1]),
                     reads=[f"d_osb0{s_}", "d_rec0"], writes=["d_t1"])
                P.op("dve", lambda s_=s_: nc.vector.scalar_tensor_tensor(
                    o[:], osb[1][s_][:, 0:256], rec[:, 1:2], t1[:], ALU.mult, ALU.add),
                    reads=[f"d_osb1{s_}", "d_rec1", "d_t1"], writes=["d_o"])
                P.op("act", lambda: nc.scalar.activation(sq[:], o[:], AF.Square), reads=["d_o"], writes=["d_sq"])
                P.op("dve", lambda: nc.vector.reduce_sum(ss[:], sq[:], axis=AX.X), reads=["d_sq"], writes=["d_ss"])
                rsqrt_col(P, ss[:], "d_ss", ss[:], "d_ss", 1e-5, 1.0 / 256.0)
                P.op("dve", lambda ni=ni: nc.vector.scalar_tensor_tensor(
                    on[ni][:], o[:], ss[:], gainb[:], ALU.mult, ALU.mult),
                    reads=["d_o", "d_ss", "d_gainb"], writes=[f"d_on{ni}"])
                for eh in range(2):
                    P.op("pe", lambda ni=ni, eh=eh: nc.tensor.transpose(
                        ps_t[:, eh * 128:(eh + 1) * 128], on[ni][:, eh * 128:(eh + 1) * 128], k.identb[:]),
                        reads=[f"d_on{ni}", "k_identb"], writes=["d_pst"])
                P.op("act", lambda ni=ni: nc.scalar.copy(ast[ni][:], ps_t[:].rearrange("p (a b) -> p a b", b=128)),
                     reads=["d_pst"], writes=[f"d_ast{ni}"])
                tq = q0 + s_ * 128
                P.dma("sp", g.attnT[rows, tq:tq + 128].rearrange("(eh p) t -> p eh t", p=128), ast[ni][:],
                      reads=[f"d_ast{ni}"], writes=["D:attnT"], key=(f"d_ast{ni}", "st"))

        for i in range(min(LOOK, len(steps))):
            score(i)
        for i, (q0, kts, c, kt) in enumerate(steps):
            if i + LOOK < len(steps):
                score(i + LOOK)
            si = i % 3
            pi = i % 4
            P.op("act", lambda si=si, pi=pi: nc.scalar.activation(
                pT[pi][:], ps_s[si][:, 0:256], AF.Exp, scale=scale),
                reads=[f"d_pss{si}"], writes=[f"d_pT{pi}"])
            for s_ in range(2):
                P.op("pe", lambda pi=pi, c=c, s_=s_, kt=kt, kts=kts: nc.tensor.matmul(
                    ps_o[c][s_][:, 0:257], lhsT=pT[pi][:, s_ * 128:(s_ + 1) * 128], rhs=vau[hi][:, kt, :],
                    start=(kt == kts[0]), stop=(kt == kts[-1])),
                    reads=[f"d_pT{pi}", f"d_va{hi}", f"d_va{hi}o"], writes=[f"d_pso{c}{s_}"])
            if c == 1 and kt == kts[-1]:
                epilogue(q0)
    P.pop()


MAGIC = 12582912.0
CW1 = 6.28125
CW2 = TWO_PI - CW1
SINSC = 0.999998
S5P = [(0, 512), (512, 512), (1024, 512), (1536, 512), (2048, 256)]


def fap(t, cols):
    a = t.ap
    return bass.AP(t.tensor, t.offset, [list(a[0]), list(a[1]), [0, 16]])


def phase_s5(P, g, k, l):
    nc = P.nc
    P.push()
    Wb = [[P.sbuf(f"e_Wb{d}{ri}", [128, 32, 128], BF16) for ri in range(2)] for d in range(2)]
    Cm = [[P.sbuf(f"e_Cm{d}{ri}", [128, 32, 128], BF16) for ri in range(2)] for d in range(2)]
    rr = P.sbuf("e_rr", [128, 64], F32)
    th = P.sbuf("e_th", [128, 64], F32)
    th2 = P.sbuf("e_th2", [128, 64], F32)
    ec = P.sbuf("e_ec", [128, 64], F32)
    es = P.sbuf("e_es", [128, 64], F32)
    nes = P.sbuf("e_nes", [128, 64], F32)
    P.push()
    lraw = P.sbuf("e_lraw", [32, 4, 128], F32)
    lam = P.sbuf("e_lam", [128, 4, 32], F32)
    ldt = P.sbuf("e_ldt", [128, 64], F32)
    psT = [P.psum(f"e_psT{i}", [128, 512]) for i in range(2)]
    for d in range(2):
        P.dma("sp", lraw[:, d, :], g.ssm_lam_re[l, d].rearrange("g p -> (g p)").rearrange("(j q) -> j q", q=128),
              writes=["e_lraw"], key=("e_lraw", "ld"))
        P.dma("sp", lraw[:, 2 + d, :], g.ssm_lam_im[l, d].rearrange("g p -> (g p)").rearrange("(j q) -> j q", q=128),
              writes=["e_lraw"], key=("e_lraw", "ld"))
        for gl in range(2):
            src = g.ssm_log_dt[l, d]
            P.dma("sp", ldt[gl * 64:(gl + 1) * 64, d * 32:(d + 1) * 32],
                  bass.AP(src.tensor, src.offset + gl, [[0, 64], [2, 32]]),
                  writes=["e_ldt"], key=("e_ldt", "ld"), allow_slow_non_contiguous=True)
    for i in range(4):
        P.op("pe", lambda i=i: nc.tensor.transpose(psT[0][:, i * 32:(i + 1) * 32], lraw[:, i, :], k.ident[0:32, 0:32]),
             reads=["e_lraw", "k_ident"], writes=["e_psT0"])
    P.op("act", lambda: nc.scalar.copy(lam[:].rearrange("p a b -> p (a b)"), psT[0][:, 0:128]),
         reads=["e_psT0"], writes=["e_lam"])
    lre = lam[:, 0:2, :].rearrange("p a b -> p (a b)")
    lim = lam[:, 2:4, :].rearrange("p a b -> p (a b)")
    tA = [P.sbuf(f"e_tA{i}", [128, 64], F32) for i in range(8)]

    def dv(fn, rd, wr):
        P.op("dve", fn, reads=rd, writes=wr)

    P.op("act", lambda: nc.scalar.activation(ldt[:], ldt[:], AF.Exp), reads=["e_ldt"], writes=["e_ldt"])
    dv(lambda: nc.vector.tensor_tensor(tA[0][:], lre, ldt[:], ALU.mult), ["e_lam", "e_ldt"], ["e_tA0"])
    P.op("act", lambda: nc.scalar.activation(rr[:], tA[0][:], AF.Exp), reads=["e_tA0"], writes=["e_rr"])
    dv(lambda: nc.vector.tensor_tensor(th[:], lim, ldt[:], ALU.mult), ["e_lam", "e_ldt"], ["e_th"])
    dv(lambda: nc.vector.tensor_scalar(tA[1][:], th[:], 1.0 / TWO_PI, MAGIC, ALU.mult, ALU.add), ["e_th"], ["e_tA1"])
    dv(lambda: nc.vector.tensor_scalar_add(tA[1][:], tA[1][:], -MAGIC), ["e_tA1"], ["e_tA1"])
    dv(lambda: nc.vector.scalar_tensor_tensor(th[:], tA[1][:], -CW1, th[:], ALU.mult, ALU.add), ["e_tA1", "e_th"], ["e_th"])
    dv(lambda: nc.vector.scalar_tensor_tensor(th[:], tA[1][:], -CW2, th[:], ALU.mult, ALU.add), ["e_tA1", "e_th"], ["e_th"])
    dv(lambda: nc.vector.tensor_scalar_mul(th2[:], th[:], 1.0 / TWO_PI), ["e_th"], ["e_th2"])
    dv(lambda: nc.vector.tensor_scalar(tA[1][:], th[:], 512.0 / TWO_PI, MAGIC, ALU.mult, ALU.add), ["e_th"], ["e_tA1"])
    dv(lambda: nc.vector.tensor_scalar_add(tA[1][:], tA[1][:], -MAGIC), ["e_tA1"], ["e_tA1"])
    dv(lambda: nc.vector.tensor_scalar_mul(tA[7][:], th[:], 512.0), ["e_th"], ["e_tA7"])
    dv(lambda: nc.vector.scalar_tensor_tensor(tA[7][:], tA[1][:], -CW1, tA[7][:], ALU.mult, ALU.add), ["e_tA1", "e_tA7"], ["e_tA7"])
    dv(lambda: nc.vector.scalar_tensor_tensor(tA[7][:], tA[1][:], -CW2, tA[7][:], ALU.mult, ALU.add), ["e_tA1", "e_tA7"], ["e_tA7"])
    dv(lambda: nc.vector.scalar_tensor_tensor(tA[1][:], tA[7][:], -1.0, tA[7][:], ALU.mult, ALU.max), ["e_tA7"], ["e_tA1"])
    P.op("act", lambda: nc.scalar.activation(es[:], tA[7][:], AF.Sin, scale=SINSC), reads=["e_tA7"], writes=["e_es"])
    P.op("act", lambda: nc.scalar.activation(ec[:], tA[1][:], AF.Sin, scale=-SINSC, bias=math.pi / 2), reads=["e_tA1"], writes=["e_ec"])
    dv(lambda: nc.vector.tensor_scalar_mul(nes[:], es[:], -1.0), ["e_es"], ["e_nes"])
    dv(lambda: nc.vector.scalar_tensor_tensor(tA[2][:], th[:], -1.0, th[:], ALU.mult, ALU.max), ["e_th"], ["e_tA2"])
    P.op("act", lambda: nc.scalar.activation(tA[3][:], th[:], AF.Sin, scale=SINSC), reads=["e_th"], writes=["e_tA3"])
    P.op("act", lambda: nc.scalar.activation(tA[2][:], tA[2][:], AF.Sin, scale=-SINSC, bias=math.pi / 2),
         reads=["e_tA2"], writes=["e_tA2"])
    dv(lambda: nc.vector.tensor_tensor(tA[2][:], tA[2][:], rr[:], ALU.mult), ["e_tA2", "e_rr"], ["e_tA2"])
    dv(lambda: nc.vector.tensor_scalar_add(tA[2][:], tA[2][:], -1.0), ["e_tA2"], ["e_tA2"])
    dv(lambda: nc.vector.tensor_tensor(tA[3][:], tA[3][:], rr[:], ALU.mult), ["e_tA3", "e_rr"], ["e_tA3"])
    dv(lambda: nc.vector.tensor_tensor(tA[4][:], lre, lre, ALU.mult), ["e_lam"], ["e_tA4"])
    dv(lambda: nc.vector.tensor_tensor(tA[5][:], lim, lim, ALU.mult), ["e_lam"], ["e_tA5"])
    dv(lambda: nc.vector.tensor_tensor(tA[4][:], tA[4][:], tA[5][:], ALU.add), ["e_tA4", "e_tA5"], ["e_tA4"])
    dv(lambda: nc.vector.reciprocal(tA[4][:], tA[4][:]), ["e_tA4"], ["e_tA4"])
    dv(lambda: nc.vector.tensor_tensor(tA[5][:], tA[2][:], lre, ALU.mult), ["e_tA2", "e_lam"], ["e_tA5"])
    dv(lambda: nc.vector.tensor_tensor(tA[6][:], tA[3][:], lim, ALU.mult), ["e_tA3", "e_lam"], ["e_tA6"])
    dv(lambda: nc.vector.tensor_tensor(tA[5][:], tA[5][:], tA[6][:], ALU.add), ["e_tA5", "e_tA6"], ["e_tA5"])
    dv(lambda: nc.vector.tensor_tensor(tA[5][:], tA[5][:], tA[4][:], ALU.mult), ["e_tA5", "e_tA4"], ["e_tA5"])
    dv(lambda: nc.vector.tensor_tensor(tA[6][:], tA[3][:], lre, ALU.mult), ["e_tA3", "e_lam"], ["e_tA6"])
    dv(lambda: nc.vector.tensor_tensor(tA[7][:], tA[2][:], lim, ALU.mult), ["e_tA2", "e_lam"], ["e_tA7"])
    dv(lambda: nc.vector.tensor_tensor(tA[6][:], tA[6][:], tA[7][:], ALU.subtract), ["e_tA6", "e_tA7"], ["e_tA6"])
    dv(lambda: nc.vector.tensor_tensor(tA[6][:], tA[6][:], tA[4][:], ALU.mult), ["e_tA6", "e_tA4"], ["e_tA6"])
    qre, qim = tA[5], tA[6]

    braw = P.sbuf("e_braw", [128, 2, 32, 16], F32)
    bb = P.sbuf("e_bb", [128, 2, 32, 16], F32)
    bt = [P.sbuf(f"e_bt{i}", [128, 32, 16], F32) for i in range(2)]
    E2 = P.sbuf("e_E2", [128, 32, 128], F32)
    Y = P.sbuf("e_Y", [128, 8, 64], F32)
    cmask = P.sbuf("e_cmask", [128, 8], F32)
    P.dma("sp", cmask[:], g.cst[:, C_CMASK:C_CMASK + 8], writes=["e_cmask"], key=("e_cmask", "ld"))
    tcnt = [0]

    def transposes(src, skey, dst, dkey, neg):
        for j0 in range(0, 32, 4):
            pi = tcnt[0] % 2
            tcnt[0] += 1
            for jj in range(4):
                P.op("pe", lambda jj=jj, j0=j0, pi=pi: nc.tensor.transpose(
                    psT[pi][:, jj * 128:(jj + 1) * 128], src[:, j0 + jj, :], k.ident[:]),
                    reads=[skey, "k_ident"], writes=[f"e_psT{pi}"])
            o = dst[:, j0:j0 + 4, :].rearrange("p a b -> p (a b)")
            if neg:
                P.op("act", lambda pi=pi, o=o: nc.scalar.mul(o, psT[pi][:], -1.0), reads=[f"e_psT{pi}"], writes=[dkey])
            else:
                P.op("act", lambda pi=pi, o=o: nc.scalar.copy(o, psT[pi][:]), reads=[f"e_psT{pi}"], writes=[dkey])

    for d in range(2):
        for ri, src in enumerate((g.ssm_b_re, g.ssm_b_im)):
            P.dma("sp", braw[:, ri, :, :], src[l, d].rearrange("g p h -> (g p) h").rearrange("(j q) h -> q j h", q=128),
                  writes=["e_braw"], key=("e_braw", "ld"))
        qr = fap(qre[:, d * 32:(d + 1) * 32], None)
        qi = fap(qim[:, d * 32:(d + 1) * 32], None)
        dv(lambda: nc.vector.tensor_tensor(bt[0][:], braw[:, 0, :, :], qr, ALU.mult), ["e_braw", "e_tA5"], ["e_bt0"])
        dv(lambda: nc.vector.tensor_tensor(bt[1][:], braw[:, 1, :, :], qi, ALU.mult), ["e_braw", "e_tA6"], ["e_bt1"])
        dv(lambda: nc.vector.tensor_tensor(bb[:, 0, :, :], bt[0][:], bt[1][:], ALU.subtract), ["e_bt0", "e_bt1"], ["e_bb0"])
        dv(lambda: nc.vector.tensor_tensor(bt[0][:], braw[:, 1, :, :], qr, ALU.mult), ["e_braw", "e_tA5"], ["e_bt0"])
        dv(lambda: nc.vector.tensor_tensor(bt[1][:], braw[:, 0, :, :], qi, ALU.mult), ["e_braw", "e_tA6"], ["e_bt1"])
        dv(lambda: nc.vector.tensor_tensor(bb[:, 1, :, :], bt[0][:], bt[1][:], ALU.add), ["e_bt0", "e_bt1"], ["e_bb1"])
        for ri in range(2):
            P.op("pool", lambda: nc.gpsimd.memset(E2[:], 0.0), reads=[], writes=["e_E2"])
            for jj in range(4):
                for gl in range(2):
                    pr = slice(gl * 64, (gl + 1) * 64)
                    c0 = 32 * jj + 16 * gl
                    dv(lambda pr=pr, c0=c0, jj=jj, ri=ri: nc.vector.tensor_copy(
                        E2[pr, jj::4, c0:c0 + 16], bb[pr, ri, jj::4, :]), [f"e_bb{ri}", "e_E2"], ["e_E2"])
            transposes(E2, "e_E2", Wb[d][ri], f"e_Wb{d}{ri}", False)
        for ri, src in enumerate((g.ssm_c_re, g.ssm_c_im)):
            P.dma("sp", Y[:], src[l, d].rearrange("g h p -> (g h) p").rearrange("(c q) p -> q c p", q=128),
                  writes=["e_Y"], key=("e_Y", "ld"))
            for jj in range(4):
                for gl in range(2):
                    m = jj * 2 + gl
                    dv(lambda jj=jj, gl=gl, m=m: nc.vector.tensor_scalar_mul(
                        E2[:, jj::4, gl * 64:(gl + 1) * 64], Y[:], cmask[:, m:m + 1]),
                        ["e_Y", "e_cmask", "e_E2"], ["e_E2"])
            transposes(E2, "e_E2", Cm[d][ri], f"e_Cm{d}{ri}", ri == 1)
    P.pop()

    P.push()
    uT = P.sbuf("e_uT", [128, 8, T], BF16)
    utmp = P.sbuf("e_utmp", [128, T], BF16)
    iota = P.sbuf("e_iota", [128, T], F32)
    carry = P.sbuf("e_carry", [128, 2, 64], F32)
    P.dma("pool", uT[:], g.uT.rearrange("(c p) t -> p c t", p=128), reads=["D:uT"], writes=["e_uT"], key=("e_uT", "ld"))
    P.dma("sp", iota[:], g.cst[:, C_IOTA:C_IOTA + T], writes=["e_iota"], key=("e_iota", "ld"))

    def mk(n, name, dt=F32):
        return [P.sbuf(f"e_{name}{i}", [128, 512], dt) for i in range(n)]

    ps_b = [[P.psum(f"e_psb{i}{ri}", [128, 512]) for ri in range(2)] for i in range(3)]
    ps_y = [P.psum(f"e_psy{i}", [128, 512]) for i in range(2)]
    A, kM, ab = mk(1, "A"), mk(1, "kM"), mk(1, "ab")
    sn, cs = mk(8, "sn"), mk(8, "cs")
    w1, w2, br, bi = mk(2, "w1"), mk(2, "w2"), mk(2, "br"), mk(2, "bi")
    gr, gi = mk(3, "gr"), mk(3, "gi")
    p1, p2 = mk(2, "p1"), mk(2, "p2")
    hr, hi_ = mk(3, "hr", BF16), mk(3, "hi", BF16)
    ystg = [P.sbuf(f"e_ystg{i}", [128, 512], F32) for i in range(2)]
    ct = P.sbuf("e_ct", [128, 2, 64], F32)
    its = []
    for d in range(2):
        for c in range(8):
            for pi_, (t0, tl) in enumerate(S5P):
                for jj in range(4):
                    its.append((d, c, pi_, t0, tl, jj))

    def dv(fn, rd, wr):
        P.op("dve", fn, reads=rd, writes=wr)

    def tb(d, c, jj):
        return ((d * 8 + c) % 2) * 4 + jj

    def S0(i):
        d, c, pi_, t0, tl, jj = its[i]
        if c == 0 and pi_ == 0 and jj == 0 and d == 1:
            for cc in range(8):
                P.op("pool", lambda cc=cc: nc.gpsimd.tensor_copy(utmp[:, 0:CTXL], rev_free(uT[:, cc, 0:CTXL])),
                     reads=["e_uT"], writes=["e_utmp"])
                P.op("pool", lambda cc=cc: nc.gpsimd.tensor_copy(utmp[:, CTXL:T], rev_free(uT[:, cc, CTXL:T])),
                     reads=["e_uT"], writes=["e_utmp"])
                P.op("pool", lambda cc=cc: nc.gpsimd.tensor_copy(uT[:, cc, :], utmp[:]), reads=["e_utmp"], writes=["e_uT"])
        j = 4 * c + jj
        col = d * 32 + j
        b3 = i % 3
        for ri in range(2):
            P.op("pe", lambda ri=ri: nc.tensor.matmul(
                ps_b[b3][ri][:, :tl], lhsT=Wb[d][ri][:, j, :], rhs=uT[:, c, t0:t0 + tl], start=True, stop=True),
                reads=[f"e_Wb{d}{ri}", "e_uT"], writes=[f"e_psb{b3}{ri}"])
        if pi_ != 0:
            return
        k8 = tb(d, c, jj)
        P.op("act", lambda: nc.scalar.activation(A[0][:], iota[:, 0:512], AF.Identity, scale=th[:, col:col + 1]),
             reads=["e_iota", "e_th"], writes=["e_A0"])
        P.op("act", lambda: nc.scalar.activation(kM[0][:], iota[:, 0:512], AF.Identity, scale=th2[:, col:col + 1], bias=MAGIC),
             reads=["e_iota", "e_th2"], writes=["e_kM0"])
        P.op("act", lambda: nc.scalar.activation(kM[0][:], kM[0][:], AF.Identity, bias=-MAGIC), reads=["e_kM0"], writes=["e_kM0"])
        dv(lambda: nc.vector.scalar_tensor_tensor(A[0][:], kM[0][:], -CW1, A[0][:], ALU.mult, ALU.add), ["e_kM0", "e_A0"], ["e_A0"])
        dv(lambda: nc.vector.scalar_tensor_tensor(A[0][:], kM[0][:], -CW2, A[0][:], ALU.mult, ALU.add), ["e_kM0", "e_A0"], ["e_A0"])
        P.op("act", lambda: nc.scalar.activation(ab[0][:], A[0][:], AF.Abs), reads=["e_A0"], writes=["e_ab0"])
        P.op("act", lambda: nc.scalar.activation(sn[k8][:], A[0][:], AF.Sin, scale=SINSC), reads=["e_A0"], writes=[f"e_sn{k8}"])
        P.op("act", lambda: nc.scalar.activation(cs[k8][:], ab[0][:], AF.Sin, scale=-SINSC, bias=math.pi / 2),
             reads=["e_ab0"], writes=[f"e_cs{k8}"])

    def S1(i):
        d, c, pi_, t0, tl, jj = its[i]
        col = d * 32 + 4 * c + jj
        b3, b2 = i % 3, i % 2
        k8 = tb(d, c, jj)
        pr_, pi2 = ps_b[b3][0], ps_b[b3][1]
        kr_, ki_ = f"e_psb{b3}0", f"e_psb{b3}1"
        dv(lambda: nc.vector.tensor_tensor(w1[b2][:, :tl], pr_[:, :tl], cs[k8][:, :tl], ALU.mult), [kr_, f"e_cs{k8}"], [f"e_w1{b2}"])
        dv(lambda: nc.vector.tensor_tensor(w2[b2][:, :tl], pi2[:, :tl], sn[k8][:, :tl], ALU.mult), [ki_, f"e_sn{k8}"], [f"e_w2{b2}"])
        dv(lambda: nc.vector.tensor_tensor(br[b2][:, :tl], w1[b2][:, :tl], w2[b2][:, :tl], ALU.add), [f"e_w1{b2}", f"e_w2{b2}"], [f"e_br{b2}"])
        dv(lambda: nc.vector.tensor_tensor(w1[b2][:, :tl], pi2[:, :tl], cs[k8][:, :tl], ALU.mult), [ki_, f"e_cs{k8}"], [f"e_w1{b2}"])
        dv(lambda: nc.vector.tensor_tensor(w2[b2][:, :tl], pr_[:, :tl], sn[k8][:, :tl], ALU.mult), [kr_, f"e_sn{k8}"], [f"e_w2{b2}"])
        dv(lambda: nc.vector.tensor_tensor(bi[b2][:, :tl], w1[b2][:, :tl], w2[b2][:, :tl], ALU.subtract), [f"e_w1{b2}", f"e_w2{b2}"], [f"e_bi{b2}"])
        for ri, (gg_, bx, gk, bk) in enumerate(((gr, br, "e_gr", "e_br"), (gi, bi, "e_gi", "e_bi"))):
            init = 0.0 if pi_ == 0 else carry[:, ri, col:col + 1]
            dv(lambda gg_=gg_, bx=bx, init=init: nc.vector.tensor_tensor_scan(
                gg_[b3][:, :tl], rr[:, col:col + 1].to_broadcast([128, tl]), bx[b2][:, :tl], init, ALU.mult, ALU.add),
                [f"{bk}{b2}", "e_rr", ("e_carry", col)], [f"{gk}{b3}"])
        if pi_ < len(S5P) - 1:
            cr_, ci_ = gr[b3][:, tl - 1:tl], gi[b3][:, tl - 1:tl]
            cc_ = slice(col, col + 1)
            P.op("act", lambda: nc.scalar.activation(ct[:, 0, cc_], cr_, AF.Identity, scale=ec[:, cc_]),
                 reads=[f"e_gr{b3}", "e_ec"], writes=[("e_ct", col)])
            P.op("act", lambda: nc.scalar.activation(ct[:, 1, cc_], ci_, AF.Identity, scale=ec[:, cc_]),
                 reads=[f"e_gi{b3}", "e_ec"], writes=[("e_ct", col)])
            P.op("act", lambda: nc.scalar.activation(carry[:, 0, cc_], ci_, AF.Identity, scale=nes[:, cc_], bias=ct[:, 0, cc_]),
                 reads=[f"e_gi{b3}", "e_nes", ("e_ct", col)], writes=[("e_carry", col)])
            P.op("act", lambda: nc.scalar.activation(carry[:, 1, cc_], cr_, AF.Identity, scale=es[:, cc_], bias=ct[:, 1, cc_]),
                 reads=[f"e_gr{b3}", "e_es", ("e_ct", col)], writes=[("e_carry", col)])

    def S2(i):
        d, c, pi_, t0, tl, jj = its[i]
        b3, b2 = i % 3, i % 2
        k8 = tb(d, c, jj)
        pl = lambda fn, rd, wr: P.op("pool", fn, reads=rd, writes=wr)
        pl(lambda: nc.gpsimd.tensor_tensor(p1[b2][:, :tl], gr[b3][:, :tl], cs[k8][:, :tl], ALU.mult), [f"e_gr{b3}", f"e_cs{k8}"], [f"e_p1{b2}"])
        pl(lambda: nc.gpsimd.tensor_tensor(p2[b2][:, :tl], gi[b3][:, :tl], sn[k8][:, :tl], ALU.mult), [f"e_gi{b3}", f"e_sn{k8}"], [f"e_p2{b2}"])
        pl(lambda: nc.gpsimd.tensor_tensor(hr[b3][:, :tl], p1[b2][:, :tl], p2[b2][:, :tl], ALU.subtract), [f"e_p1{b2}", f"e_p2{b2}"], [f"e_hr{b3}"])
        pl(lambda: nc.gpsimd.tensor_tensor(p1[b2][:, :tl], gi[b3][:, :tl], cs[k8][:, :tl], ALU.mult), [f"e_gi{b3}", f"e_cs{k8}"], [f"e_p1{b2}"])
        pl(lambda: nc.gpsimd.tensor_tensor(p2[b2][:, :tl], gr[b3][:, :tl], sn[k8][:, :tl], ALU.mult), [f"e_gr{b3}", f"e_sn{k8}"], [f"e_p2{b2}"])
        pl(lambda: nc.gpsimd.tensor_tensor(hi_[b3][:, :tl], p1[b2][:, :tl], p2[b2][:, :tl], ALU.add), [f"e_p1{b2}", f"e_p2{b2}"], [f"e_hi{b3}"])

    def S3(i):
        d, c, pi_, t0, tl, jj = its[i]
        j = 4 * c + jj
        b3 = i % 3
        yi = (i // 4) % 2
        P.op("pe", lambda: nc.tensor.matmul(ps_y[yi][:, :tl], lhsT=Cm[d][0][:, j, :], rhs=hr[b3][:, :tl], start=(jj == 0), stop=False),
             reads=[f"e_Cm{d}0", f"e_hr{b3}"], writes=[f"e_psy{yi}"])
        P.op("pe", lambda: nc.tensor.matmul(ps_y[yi][:, :tl], lhsT=Cm[d][1][:, j, :], rhs=hi_[b3][:, :tl], start=False, stop=(jj == 3)),
             reads=[f"e_Cm{d}1", f"e_hi{b3}"], writes=[f"e_psy{yi}"])
        if jj == 3:
            P.op("act", lambda: nc.scalar.copy(ystg[yi][:, :tl], ps_y[yi][:, :tl]), reads=[f"e_psy{yi}"], writes=[f"e_ystg{yi}"])
            P.dma("sp", g.yfb[d, c * 128:(c + 1) * 128, t0:t0 + tl], ystg[yi][:, :tl], reads=[f"e_ystg{yi}"],
                  writes=["D:yfb"], key=(f"e_ystg{yi}", "st"))

    n = len(its)
    half = n // 2
    for s_ in range(n + 3):
        if s_ < n:
            S0(s_)
        if 0 <= s_ - 1 < n:
            S1(s_ - 1)
        if 0 <= s_ - 2 < n:
            S2(s_ - 2)
        if 0 <= s_ - 3 < n:
            S3(s_ - 3)
    P.pop()
    P.pop()


def phase_s5_glu(P, g, k, l):
    nc = P.nc
    P.push()
    gT = P.sbuf("f_gT", [128, 8, T], BF16)
    Dc = P.sbuf("f_Dc", [128, 8], F32)
    bgl = P.sbuf("f_bgl", [128, 8], F32)
    wgl = P.sbuf("f_wgl", [128, 8, BW], BF16)
    P.dma("sp", Dc[:], g.ssm_d[l].rearrange("(c p) -> p c", p=128), writes=["f_Dc"], key=("f_Dc", "ld"),
          allow_slow_non_contiguous=True)
    P.dma("sp", bgl[:], g.b_ssm_glu[l].rearrange("(c p) -> p c", p=128), writes=["f_bgl"], key=("f_bgl", "ld"),
          allow_slow_non_contiguous=True)
    P.dma("pool", wgl[:], g.w_ssm_glu[l].rearrange("(kk p) n -> p kk n", p=128), writes=["f_wgl"], key=("f_wgl", "ld"))
    yf = [P.sbuf(f"f_yf{i}", [128, 512], F32) for i in range(2)]
    yb = [P.sbuf(f"f_yb{i}", [128, 512], F32) for i in range(2)]
    uu = [P.sbuf(f"f_u{i}", [128, 512], F32) for i in range(2)]
    n = 0
    for c in range(8):
        rows = slice(c * 128, (c + 1) * 128)
        for (t0, tl) in CH:
            i = n % 2
            n += 1
            tau0 = (CTXL - t0 - tl) if t0 < CTXL else (T + CTXL - t0 - tl)
            P.dma("sp", yf[i][:, :tl], g.yfb[0, rows, t0:t0 + tl], reads=["D:yfb"], writes=[f"f_yf{i}"], key=(f"f_yf{i}", "ld"))
            P.dma("sp", yb[i][:, :tl], g.yfb[1, rows, tau0:tau0 + tl], reads=["D:yfb"], writes=[f"f_yb{i}"], key=(f"f_yb{i}", "ld"))
            P.dma("sp", uu[i][:, :tl], g.uT[rows, t0:t0 + tl], reads=["D:uT"], writes=[f"f_u{i}"], key=(f"f_u{i}", "ld"))
            P.op("dve", lambda i=i, tl=tl: nc.vector.tensor_tensor(yf[i][:, :tl], yf[i][:, :tl], rev_free(yb[i][:, :tl]), ALU.add),
                 reads=[f"f_yf{i}", f"f_yb{i}"], writes=[f"f_yf{i}"])
            P.op("dve", lambda i=i, tl=tl, c=c: nc.vector.scalar_tensor_tensor(
                yf[i][:, :tl], uu[i][:, :tl], Dc[:, c:c + 1], yf[i][:, :tl], ALU.mult, ALU.add),
                reads=[f"f_u{i}", "f_Dc", f"f_yf{i}"], writes=[f"f_yf{i}"])
            P.op("act", lambda i=i, tl=tl, c=c, t0=t0: nc.scalar.activation(gT[:, c, t0:t0 + tl], yf[i][:, :tl], AF.Gelu),
                 reads=[f"f_yf{i}"], writes=[("f_gT", c, t0)])
    ps = [P.psum(f"f_ps{i}", [128, 512]) for i in range(2)]
    sg = [P.sbuf(f"f_sg{i}", [128, 512], F32) for i in range(2)]
    ob = [P.sbuf(f"f_ob{i}", [128, 512], BF16) for i in range(2)]
    n = 0
    for co in range(8):
        for (t0, tl) in CH:
            i = n % 2
            n += 1
            for kk in range(8):
                P.op("pe", lambda kk=kk, i=i, co=co, t0=t0, tl=tl: nc.tensor.matmul(
                    ps[i][:, :tl], lhsT=wgl[:, kk, co * 128:(co + 1) * 128], rhs=gT[:, kk, t0:t0 + tl],
                    start=(kk == 0), stop=(kk == 7)),
                    reads=["f_wgl", ("f_gT", kk, t0)], writes=[f"f_ps{i}"])
            P.op("act", lambda i=i, tl=tl, co=co: nc.scalar.activation(sg[i][:, :tl], ps[i][:, :tl], AF.Sigmoid, bias=bgl[:, co:co + 1]),
                 reads=[f"f_ps{i}", "f_bgl"], writes=[f"f_sg{i}"])
            P.op("dve", lambda i=i, tl=tl, co=co, t0=t0: nc.vector.tensor_tensor(ob[i][:, :tl], gT[:, co, t0:t0 + tl], sg[i][:, :tl], ALU.mult),
                 reads=[("f_gT", co, t0), f"f_sg{i}"], writes=[f"f_ob{i}"])
            P.dma("sp", g.ssmT[co * 128:(co + 1) * 128, t0:t0 + tl], ob[i][:, :tl], reads=[f"f_ob{i}"], writes=["D:ssmT"],
                  key=(f"f_ob{i}", "st"))
    P.pop()


YOFF_C = 15
YOFF_X = 15 + CTXL + 30
YW = YOFF_X + SEQ + 15


def phase_conv(P, g, k, l):
    nc = P.nc
    P.push()
    cwr = P.sbuf("g_cwr", [31, BW], F32)
    cw = P.sbuf("g_cw", [128, 8, 32], F32)
    cb = P.sbuf("g_cb", [128, 8], F32)
    lg = P.sbuf("g_lg", [128, 8], F32)
    lb = P.sbuf("g_lb", [128, 8], F32)
    pst = P.psum("g_pst", [128, 512])
    P.dma("sp", cwr[:], g.conv_w[l], writes=["g_cwr"], key=("g_cwr", "ld"))
    for nm, t_, src in (("g_cb", cb, g.conv_b), ("g_lg", lg, g.conv_ln_g), ("g_lb", lb, g.conv_ln_b)):
        P.dma("sp", t_[:], src[l].rearrange("(c p) -> p c", p=128), writes=[nm], key=(nm, "ld"), allow_slow_non_contiguous=True)
    for c in range(8):
        P.op("pe", lambda c=c: nc.tensor.transpose(pst[:, c * 32:c * 32 + 31], cwr[:, c * 128:(c + 1) * 128], k.ident[0:31, 0:31]),
             reads=["g_cwr", "k_ident"], writes=["g_pst"])
    for c in range(8):
        P.op("act", lambda c=c: nc.scalar.copy(cw[:, c, 0:31], pst[:, c * 32:c * 32 + 31]), reads=["g_pst"], writes=["g_cw"])
    ybuf = [P.sbuf(f"g_ybuf{i}", [128, YW], BF16) for i in range(2)]
    for i in range(2):
        P.op("pool", lambda i=i: nc.gpsimd.memset(ybuf[i][:], 0.0), writes=[f"g_ybuf{i}"])
    aa = [P.sbuf(f"g_aa{i}", [128, T], F32) for i in range(2)]
    sb = [P.sbuf(f"g_sb{i}", [128, T], F32) for i in range(2)]
    dg = [P.sbuf(f"g_dg{i}", [128, 31, 128], BF16) for i in range(2)]
    ps = [P.psum(f"g_ps{i}", [128, 512]) for i in range(2)]
    stg = [P.sbuf(f"g_stg{i}", [128, 512], F32) for i in range(2)]
    n = 0
    for c in range(8):
        i = c % 2
        rows = slice(c * 128, (c + 1) * 128)
        P.dma("sp", aa[i][:], g.cab[rows, :], reads=["D:cab"], writes=[f"g_aa{i}"], key=(f"g_aa{i}", "ld"))
        P.dma("sp", sb[i][:], g.cab[BW + c * 128:BW + (c + 1) * 128, :], reads=["D:cab"], writes=[f"g_sb{i}"], key=(f"g_sb{i}", "ld"))
        P.op("dve", lambda i=i: nc.vector.tensor_tensor(ybuf[i][:, YOFF_C:YOFF_C + CTXL], aa[i][:, 0:CTXL], sb[i][:, 0:CTXL], ALU.mult),
             reads=[f"g_aa{i}", f"g_sb{i}"], writes=[f"g_ybuf{i}"])
        P.op("dve", lambda i=i: nc.vector.tensor_tensor(ybuf[i][:, YOFF_X:YOFF_X + SEQ], aa[i][:, CTXL:T], sb[i][:, CTXL:T], ALU.mult),
             reads=[f"g_aa{i}", f"g_sb{i}"], writes=[f"g_ybuf{i}"])
        for tau in range(31):
            P.op("pool", lambda i=i, tau=tau, c=c: nc.gpsimd.tensor_scalar_mul(dg[i][:, tau, :], k.ident[:], cw[:, c, tau:tau + 1]),
                 reads=["k_ident", "g_cw"], writes=[f"g_dg{i}"])
        for (t0, tl) in CH:
            off = (YOFF_C + t0) if t0 < CTXL else (YOFF_X + t0 - CTXL)
            pi = n % 2
            n += 1
            for tau in range(31):
                P.op("pe", lambda i=i, tau=tau, pi=pi, off=off, tl=tl: nc.tensor.matmul(
                    ps[pi][:, :tl], lhsT=dg[i][:, tau, :], rhs=ybuf[i][:, off + tau - 15:off + tau - 15 + tl],
                    start=(tau == 0), stop=(tau == 30)),
                    reads=[f"g_dg{i}", f"g_ybuf{i}"], writes=[f"g_ps{pi}"])
            P.op("act", lambda pi=pi, tl=tl, c=c: nc.scalar.activation(stg[pi][:, :tl], ps[pi][:, :tl], AF.Identity, bias=cb[:, c:c + 1]),
                 reads=[f"g_ps{pi}", "g_cb"], writes=[f"g_stg{pi}"])
            P.dma("sp", g.convo[rows, t0:t0 + tl], stg[pi][:, :tl], reads=[f"g_stg{pi}"], writes=["D:convo"], key=(f"g_stg{pi}", "st"))
    P.pop()
    P.push()
    lg = P.sbuf("h_lg", [128, 8], F32)
    lb = P.sbuf("h_lb", [128, 8], F32)
    for nm, t_, src in (("h_lg", lg, g.conv_ln_g), ("h_lb", lb, g.conv_ln_b)):
        P.dma("sp", t_[:], src[l].rearrange("(c p) -> p c", p=128), writes=[nm], key=(nm, "ld"), allow_slow_non_contiguous=True)
    om = P.sbuf("h_om", [128, 128], F32)
    P.op("dve", lambda: nc.vector.tensor_scalar_mul(om[:], k.ones[:], 1.0 / BW), reads=["k_ones"], writes=["h_om"])
    xc = [P.sbuf(f"h_xc{i}", [128, 8, 512], F32) for i in range(2)]
    sq = P.sbuf("h_sq", [128, 8, 512], F32)
    pm = P.psum("h_pm", [128, 512])
    pq = P.psum("h_pq", [128, 512])
    mean = P.sbuf("h_mean", [128, 512], F32)
    rstd = P.sbuf("h_rstd", [128, 512], F32)
    xn = [P.sbuf(f"h_xn{i}", [128, 512], F32) for i in range(2)]
    ob = [P.sbuf(f"h_ob{i}", [128, 512], BF16) for i in range(2)]
    n = 0
    for ci, (t0, tl) in enumerate(CH):
        i = ci % 2
        P.dma("sp", xc[i][:, :, :tl], g.convo[:, t0:t0 + tl].rearrange("(c p) t -> p c t", p=128), reads=["D:convo"],
              writes=[f"h_xc{i}"], key=(f"h_xc{i}", "ld"))
        P.op("act", lambda i=i, tl=tl: nc.scalar.activation(sq[:, :, :tl], xc[i][:, :, :tl], AF.Square), reads=[f"h_xc{i}"], writes=["h_sq"])
        for c in range(8):
            P.op("pe", lambda c=c, i=i, tl=tl: nc.tensor.matmul(pm[:, :tl], lhsT=om[:], rhs=xc[i][:, c, :tl], start=(c == 0), stop=(c == 7)),
                 reads=["h_om", f"h_xc{i}"], writes=["h_pm"])
        for c in range(8):
            P.op("pe", lambda c=c, tl=tl: nc.tensor.matmul(pq[:, :tl], lhsT=om[:], rhs=sq[:, c, :tl], start=(c == 0), stop=(c == 7)),
                 reads=["h_om", "h_sq"], writes=["h_pq"])
        P.op("act", lambda tl=tl: nc.scalar.copy(mean[:, :tl], pm[:, :tl]), reads=["h_pm"], writes=["h_mean"])
        P.op("dve", lambda tl=tl: nc.vector.tensor_tensor(rstd[:, :tl], mean[:, :tl], mean[:, :tl], ALU.mult), reads=["h_mean"], writes=["h_rstd"])
        P.op("dve", lambda tl=tl: nc.vector.tensor_tensor(rstd[:, :tl], pq[:, :tl], rstd[:, :tl], ALU.subtract), reads=["h_pq", "h_rstd"], writes=["h_rstd"])
        rsqrt_col(P, rstd[:, :tl], "h_rstd", rstd[:, :tl], "h_rstd", 1e-5, 1.0)
        for c in range(8):
            j = n % 2
            n += 1
            P.op("dve", lambda c=c, i=i, j=j, tl=tl: nc.vector.tensor_tensor(xn[j][:, :tl], xc[i][:, c, :tl], mean[:, :tl], ALU.subtract),
                 reads=[f"h_xc{i}", "h_mean"], writes=[f"h_xn{j}"])
            P.op("dve", lambda j=j, tl=tl: nc.vector.tensor_tensor(xn[j][:, :tl], xn[j][:, :tl], rstd[:, :tl], ALU.mult),
                 reads=[f"h_xn{j}", "h_rstd"], writes=[f"h_xn{j}"])
            P.op("act", lambda j=j, c=c, tl=tl: nc.scalar.activation(ob[j][:, :tl], xn[j][:, :tl], AF.Silu, scale=lg[:, c:c + 1], bias=lb[:, c:c + 1]),
                 reads=[f"h_xn{j}", "h_lg", "h_lb"], writes=[f"h_ob{j}"])
            P.dma("sp", g.ycnT[c * 128:(c + 1) * 128, t0:t0 + tl], ob[j][:, :tl], reads=[f"h_ob{j}"], writes=["D:ycnT"], key=(f"h_ob{j}", "st"))
    P.pop()


SCRATCH_W = {
    "wao_b": ([16, 128, 8, 128], BF16), "wso_b": ([16, 128, 8, 128], BF16), "wco_b": ([16, 128, 8, 128], BF16),
    "wout_b": ([4, 128, 16, 512], BF16), "wdn_b": ([8, 128, FT, 256], BF16),
}


def phase_precast(P, g, l):
    for nm, src in (("wao_b", g.w_att_o), ("wso_b", g.w_ssm_o), ("wco_b", g.w_conv_o)):
        dst = getattr(g, nm)
        for f in range(16):
            P.dma("pool", dst[f], src[l, :, f * 128:(f + 1) * 128].rearrange("(kk p) n -> p kk n", p=128),
                  writes=["D:" + nm], key=("precast", nm))
    for nq in range(4):
        P.dma("pool", g.wout_b[nq], g.w_out[l, :, nq * 512:(nq + 1) * 512].rearrange("(kk p) n -> p kk n", p=128),
              writes=["D:wout_b"], key=("precast", "wout"))
    for ne in range(8):
        P.dma("pool", g.wdn_b[ne], g.w_down[l, :, ne * 256:(ne + 1) * 256].rearrange("(kk p) n -> p kk n", p=128),
              writes=["D:wdn_b"], key=("precast", "wdn"))


def store_rows(g, dst, ti):
    if dst == "out":
        return g.out[(ti - 2) * 128:(ti - 1) * 128, :], "D:out"
    return getattr(g, dst)[ti * 128:(ti + 1) * 128, :], "D:" + dst


def resid_ln(P, w, pfx, xpre, xkey, gb, gkey, bb, bkey, eps=1e-5):
    nc = P.nc
    ln_stats(P, w, xpre, xkey, eps)
    P.op("act", lambda: nc.scalar.activation(xpre[:], xpre[:], AF.Identity, bias=w.nmr[:], scale=w.rstd[:]),
         reads=[xkey, w.pfx + "nmr", w.pfx + "rstd"], writes=[xkey])
    P.op("dve", lambda: nc.vector.tensor_tensor(xpre[:], xpre[:], gb[:], ALU.mult), reads=[xkey, gkey], writes=[xkey])
    P.op("dve", lambda: nc.vector.tensor_tensor(xpre[:], xpre[:], bb[:], ALU.add), reads=[xkey, bkey], writes=[xkey])


def phase_merge(P, g, k, l, src, dst, hT):
    nc = P.nc
    P.push()
    acts = [[P.sbuf(f"m_act{b}{i}", [128, 8, 512], BF16) for b in range(3)] for i in range(2)]
    wbr = [[P.sbuf(f"m_w{b}{i}", [128, 8, 128], BF16) for b in range(3)] for i in range(2)]
    gt = [P.sbuf(f"m_gt{i}", [128, 3, 512], F32) for i in range(2)]
    ps = [[P.psum(f"m_ps{b}{i}", [128, 512]) for b in range(3)] for i in range(2)]
    m1 = P.sbuf("m_m1", [128, 512], F32)
    m2 = P.sbuf("m_m2", [128, 512], F32)
    srcs = [(g.attnT, "D:attnT", g.wao_b, "D:wao_b"), (g.ssmT, "D:ssmT", g.wso_b, "D:wso_b"), (g.ycnT, "D:ycnT", g.wco_b, "D:wco_b")]
    gview = g.gate.rearrange("(gi ff p) t -> p gi ff t", gi=3, p=128)
    n = 0
    for ci, (t0, tl) in enumerate(CH):
        ai = ci % 2
        tis = list(range(t0 // 128, (t0 + tl) // 128))
        for b in range(3):
            P.dma("sp", acts[ai][b][:, :, :tl], srcs[b][0][:, t0:t0 + tl].rearrange("(c p) t -> p c t", p=128),
                  reads=[srcs[b][1]], writes=[f"m_act{b}{ai}"], key=(f"m_act{b}{ai}", "ld"))
        for f in range(16):
            i = n % 2
            n += 1
            for b in range(3):
                P.dma("sp", wbr[i][b][:], srcs[b][2][f], reads=[srcs[b][3]], writes=[f"m_w{b}{i}"], key=(f"m_w{b}{i}", "ld"))
            P.dma("sp", gt[i][:, :, :tl], gview[:, :, f, t0:t0 + tl], reads=["D:gate"], writes=[f"m_gt{i}"], key=(f"m_gt{i}", "ld"))
            for b in range(3):
                for kk in range(8):
                    P.op("pe", lambda b=b, kk=kk, i=i, ai=ai, tl=tl: nc.tensor.matmul(
                        ps[i][b][:, :tl], lhsT=wbr[i][b][:, kk, :], rhs=acts[ai][b][:, kk, :tl], start=(kk == 0), stop=(kk == 7)),
                        reads=[f"m_w{b}{i}", f"m_act{b}{ai}"], writes=[f"m_ps{b}{i}"])
            P.op("dve", lambda i=i, tl=tl: nc.vector.tensor_tensor(m1[:, :tl], ps[i][0][:, :tl], gt[i][:, 0, :tl], ALU.mult),
                 reads=[f"m_ps0{i}", f"m_gt{i}"], writes=["m_m1"])
            P.op("dve", lambda i=i, tl=tl: nc.vector.tensor_tensor(m2[:, :tl], ps[i][1][:, :tl], gt[i][:, 1, :tl], ALU.mult),
                 reads=[f"m_ps1{i}", f"m_gt{i}"], writes=["m_m2"])
            P.op("dve", lambda tl=tl: nc.vector.tensor_tensor(m1[:, :tl], m1[:, :tl], m2[:, :tl], ALU.add),
                 reads=["m_m1", "m_m2"], writes=["m_m1"])
            P.op("dve", lambda i=i, tl=tl: nc.vector.tensor_tensor(m2[:, :tl], ps[i][2][:, :tl], gt[i][:, 2, :tl], ALU.mult),
                 reads=[f"m_ps2{i}", f"m_gt{i}"], writes=["m_m2"])
            P.op("dve", lambda f=f, t0=t0, tl=tl: nc.vector.tensor_tensor(hT[:, f, t0:t0 + tl], m1[:, :tl], m2[:, :tl], ALU.add),
                 reads=["m_m1", "m_m2"], writes=[("hT", ti) for ti in tis])
    P.pop()
    P.push()
    w = alloc_lnmod(P, "n_")
    wo = [P.sbuf(f"n_wo{i}", [128, KT, 512], BF16) for i in range(2)]
    xpre = [P.sbuf(f"n_xp{i}", [128, D], F32) for i in range(4)]
    tmp = [P.sbuf(f"n_tmp{i}", [128, 512], F32) for i in range(2)]
    gb = P.sbuf("n_gb", [128, D], F32)
    l1g = P.sbuf("n_l1g", [128, D], F32)
    l1b = P.sbuf("n_l1b", [128, D], F32)
    scb = P.sbuf("n_scb", [128, D], F32)
    shb = P.sbuf("n_shb", [128, D], F32)
    ps = [P.psum(f"n_ps{i}", [128, 512]) for i in range(2)]
    P.dma("sp", l1g[:], bcast_rows(g.ln1_g[l]), writes=["n_l1g"], key=("n_l1g", "ld"))
    P.dma("sp", l1b[:], bcast_rows(g.ln1_b[l]), writes=["n_l1b"], key=("n_l1b", "ld"))
    n = 0
    sn = 0
    groups = [[0, 1]] + [list(range(2 + 4 * q, 6 + 4 * q)) for q in range(4)]
    for gi_, tis in enumerate(groups):
        r = 1 if gi_ == 0 else 0
        if gi_ in (0, 1):
            load_mod_bcast(P, g, l, r, 2, gb, "n_gb")
            load_mod_bcast(P, g, l, r, 4, scb, "n_scb", plus_one=True)
            load_mod_bcast(P, g, l, r, 3, shb, "n_shb")
        for tt, ti in enumerate(tis):
            rows, dk = stream_rows(g, src, ti)
            P.dma("sp", xpre[tt][:], rows, reads=[dk], writes=[f"n_xp{tt}"], key=(f"n_xp{tt}", "ld"))
            P.op("act", lambda tt=tt: nc.scalar.mul(xpre[tt][:], xpre[tt][:], ALPHA), reads=[f"n_xp{tt}"], writes=[f"n_xp{tt}"])
        for nq in range(4):
            si = sn % 2
            sn += 1
            P.dma("sp", wo[si][:], g.wout_b[nq], reads=["D:wout_b"], writes=[f"n_wo{si}"], key=(f"n_wo{si}", "ld"))
            for tt, ti in enumerate(tis):
                i = n % 2
                n += 1
                for kk in range(KT):
                    P.op("pe", lambda kk=kk, i=i, si=si, ti=ti: nc.tensor.matmul(
                        ps[i][:], lhsT=hT[:, kk, ti * 128:(ti + 1) * 128], rhs=wo[si][:, kk, :], start=(kk == 0), stop=(kk == KT - 1)),
                        reads=[("hT", ti), f"n_wo{si}"], writes=[f"n_ps{i}"])
                cols = slice(nq * 512, (nq + 1) * 512)
                P.op("dve", lambda i=i, cols=cols: nc.vector.tensor_tensor(tmp[i][:], ps[i][:], gb[:, cols], ALU.mult),
                     reads=[f"n_ps{i}", "n_gb"], writes=[f"n_tmp{i}"])
                P.op("dve", lambda i=i, tt=tt, cols=cols: nc.vector.tensor_tensor(xpre[tt][:, cols], xpre[tt][:, cols], tmp[i][:], ALU.add),
                     reads=[f"n_tmp{i}", f"n_xp{tt}"], writes=[f"n_xp{tt}"])
        for tt, ti in enumerate(tis):
            resid_ln(P, w, "n_", xpre[tt], f"n_xp{tt}", l1g, "n_l1g", l1b, "n_l1b")
            rows, dk = store_rows(g, dst, ti)
            P.dma("sp", rows, xpre[tt][:], reads=[f"n_xp{tt}"], writes=[dk], key=(f"n_xp{tt}", "st"))
            ln_mod_T(P, k, w, xpre[tt], f"n_xp{tt}", scb, "n_scb", shb, "n_shb", hT, "hT", ti)
    P.pop()


UOFF_C = 1
UOFF_X = 1 + CTXL + 2
UW = UOFF_X + SEQ + 1


def phase_ffn_up(P, g, k, l, hT):
    nc = P.nc
    P.push()
    fw = P.sbuf("u_fw", [128, 3, 2 * FT], F32)
    fb = P.sbuf("u_fb", [128, 2 * FT], F32)
    for tau in range(3):
        P.dma("sp", fw[:, tau, :], g.ffn_conv_w[l, tau].rearrange("(c p) -> p c", p=128), writes=["u_fw"], key=("u_fw", "ld"),
              allow_slow_non_contiguous=True)
    P.dma("sp", fb[:], g.ffn_conv_b[l].rearrange("(c p) -> p c", p=128), writes=["u_fb"], key=("u_fb", "ld"),
          allow_slow_non_contiguous=True)
    ws = [[P.sbuf(f"u_ws{h}{i}", [128, KT, 256], BF16) for h in range(2)] for i in range(2)]
    ub = [[P.sbuf(f"u_ub{h}{i}", [128, UW], F32) for h in range(2)] for i in range(2)]
    cv = [[P.sbuf(f"u_cv{h}{i}", [128, UW], F32) for h in range(2)] for i in range(2)]
    ab = [P.sbuf(f"u_ab{i}", [128, UW], BF16) for i in range(2)]
    ps = [P.psum(f"u_ps{i}", [128, 512]) for i in range(4)]
    for i in range(2):
        for h in range(2):
            P.op("pool", lambda i=i, h=h: nc.gpsimd.memset(ub[h][i][:], 0.0), writes=[f"u_ub{h}{i}"])
    pn = 0
    hkeys = [("hT", ti) for ti in range(NT)]
    for sl in range(22):
        si = sl % 2
        nt = 2 if sl < 21 else 1
        for h in range(2):
            c0 = h * DFF + sl * 256
            P.dma("pool", ws[si][h][:, :, :nt * 128], g.w_up[l, :, c0:c0 + nt * 128].rearrange("(kk p) n -> p kk n", p=128),
                  writes=[f"u_ws{h}{si}"], key=(f"u_ws{h}{si}", "ld"))
        for cc in range(nt):
            i = sl * 2 + cc
            bi = i % 2
            for h in range(2):
                for (t0, tl) in CH:
                    pi = pn % 4
                    pn += 1
                    tis = [("hT", ti) for ti in range(t0 // 128, (t0 + tl) // 128)]
                    for kk in range(KT):
                        P.op("pe", lambda kk=kk, pi=pi, h=h, t0=t0, tl=tl, cc=cc: nc.tensor.matmul(
                            ps[pi][:, :tl], lhsT=ws[si][h][:, kk, cc * 128:(cc + 1) * 128], rhs=hT[:, kk, t0:t0 + tl],
                            start=(kk == 0), stop=(kk == KT - 1)),
                            reads=tis + [f"u_ws{h}{si}"], writes=[f"u_ps{pi}"])
                    off = (UOFF_C + t0) if t0 < CTXL else (UOFF_X + t0 - CTXL)
                    P.op("act", lambda pi=pi, h=h, off=off, tl=tl: nc.scalar.copy(ub[h][bi][:, off:off + tl], ps[pi][:, :tl]),
                         reads=[f"u_ps{pi}"], writes=[f"u_ub{h}{bi}"])
                col = h * FT + i
                u_, c_ = ub[h][bi], cv[h][bi]
                uk, ck = f"u_ub{h}{bi}", f"u_cv{h}{bi}"
                P.op("dve", lambda u_=u_, c_=c_, col=col: nc.vector.tensor_scalar(
                    c_[:, 1:UW - 1], u_[:, 1:UW - 1], fw[:, 1, col:col + 1], fb[:, col:col + 1], ALU.mult, ALU.add),
                    reads=[uk, "u_fw", "u_fb"], writes=[ck])
                P.op("dve", lambda u_=u_, c_=c_, col=col: nc.vector.scalar_tensor_tensor(
                    c_[:, 1:UW - 1], u_[:, 0:UW - 2], fw[:, 0, col:col + 1], c_[:, 1:UW - 1], ALU.mult, ALU.add),
                    reads=[uk, "u_fw", ck], writes=[ck])
                P.op("dve", lambda u_=u_, c_=c_, col=col: nc.vector.scalar_tensor_tensor(
                    c_[:, 1:UW - 1], u_[:, 2:UW], fw[:, 2, col:col + 1], c_[:, 1:UW - 1], ALU.mult, ALU.add),
                    reads=[uk, "u_fw", ck], writes=[ck])
            P.op("act", lambda bi=bi: nc.scalar.activation(cv[0][bi][:, 1:UW - 1], cv[0][bi][:, 1:UW - 1], AF.Silu),
                 reads=[f"u_cv0{bi}"], writes=[f"u_cv0{bi}"])
            P.op("dve", lambda bi=bi: nc.vector.tensor_tensor(ab[bi][:, 1:UW - 1], cv[0][bi][:, 1:UW - 1], cv[1][bi][:, 1:UW - 1], ALU.mult),
                 reads=[f"u_cv0{bi}", f"u_cv1{bi}"], writes=[f"u_ab{bi}"])
            rows = slice(i * 128, (i + 1) * 128)
            P.dma("sp", g.aT[rows, 0:CTXL], ab[bi][:, UOFF_C:UOFF_C + CTXL], reads=[f"u_ab{bi}"], writes=["D:aT"], key=(f"u_ab{bi}", "st"))
            P.dma("sp", g.aT[rows, CTXL:T], ab[bi][:, UOFF_X:UOFF_X + SEQ], reads=[f"u_ab{bi}"], writes=["D:aT"], key=(f"u_ab{bi}", "st"))
    P.pop()


def phase_ffn_down(P, g, k, l, src, dst):
    nc = P.nc
    P.push()
    w = alloc_lnmod(P, "o_")
    at = [P.sbuf(f"o_at{i}", [128, FT, 512], BF16) for i in range(2)]
    wd = [P.sbuf(f"o_wd{i}", [128, FT, 256], BF16) for i in range(2)]
    xpre = [P.sbuf(f"o_xp{i}", [128, D], F32) for i in range(4)]
    tmp = [P.sbuf(f"o_tmp{i}", [128, 256], F32) for i in range(2)]
    gb = P.sbuf("o_gb", [128, D], F32)
    l2g = P.sbuf("o_l2g", [128, D], F32)
    l2b = P.sbuf("o_l2b", [128, D], F32)
    ps = [P.psum(f"o_ps{i}", [128, 512]) for i in range(2)]
    P.dma("sp", l2g[:], bcast_rows(g.ln2_g[l]), writes=["o_l2g"], key=("o_l2g", "ld"))
    P.dma("sp", l2b[:], bcast_rows(g.ln2_b[l]), writes=["o_l2b"], key=("o_l2b", "ld"))
    n = 0
    sn = 0
    an = 0
    groups = [[0, 1]] + [list(range(2 + 4 * q, 6 + 4 * q)) for q in range(4)]
    for gi_, tis in enumerate(groups):
        r = 1 if gi_ == 0 else 0
        if gi_ in (0, 1):
            load_mod_bcast(P, g, l, r, 5, gb, "o_gb")
        ai = an % 2
        an += 1
        t0 = tis[0] * 128
        tw = len(tis) * 128
        P.dma("sp", at[ai][:, :, :tw], g.aT[:, t0:t0 + tw].rearrange("(kk p) t -> p kk t", p=128), reads=["D:aT"],
              writes=[f"o_at{ai}"], key=(f"o_at{ai}", "ld"))
        for tt, ti in enumerate(tis):
            rows, dk = stream_rows(g, src, ti)
            P.dma("sp", xpre[tt][:], rows, reads=[dk], writes=[f"o_xp{tt}"], key=(f"o_xp{tt}", "ld"))
            P.op("act", lambda tt=tt: nc.scalar.mul(xpre[tt][:], xpre[tt][:], ALPHA), reads=[f"o_xp{tt}"], writes=[f"o_xp{tt}"])
        for ne in range(8):
            si = sn % 2
            sn += 1
            P.dma("sp", wd[si][:], g.wdn_b[ne], reads=["D:wdn_b"], writes=[f"o_wd{si}"], key=(f"o_wd{si}", "ld"))
            for tt, ti in enumerate(tis):
                i = n % 2
                n += 1
                for kk in range(FT):
                    P.op("pe", lambda kk=kk, i=i, si=si, tt=tt: nc.tensor.matmul(
                        ps[i][:, 0:256], lhsT=at[ai][:, kk, tt * 128:(tt + 1) * 128], rhs=wd[si][:, kk, :],
                        start=(kk == 0), stop=(kk == FT - 1)),
                        reads=[f"o_at{ai}", f"o_wd{si}"], writes=[f"o_ps{i}"])
                cols = slice(ne * 256, (ne + 1) * 256)
                P.op("dve", lambda i=i, cols=cols: nc.vector.tensor_tensor(tmp[i][:], ps[i][:, 0:256], gb[:, cols], ALU.mult),
                     reads=[f"o_ps{i}", "o_gb"], writes=[f"o_tmp{i}"])
                P.op("dve", lambda i=i, tt=tt, cols=cols: nc.vector.tensor_tensor(xpre[tt][:, cols], xpre[tt][:, cols], tmp[i][:], ALU.add),
                     reads=[f"o_tmp{i}", f"o_xp{tt}"], writes=[f"o_xp{tt}"])
        for tt, ti in enumerate(tis):
            if dst == "out" and ti < 2:
                continue
            resid_ln(P, w, "o_", xpre[tt], f"o_xp{tt}", l2g, "o_l2g", l2b, "o_l2b")
            rows, dk = store_rows(g, dst, ti)
            P.dma("sp", rows, xpre[tt][:], reads=[f"o_xp{tt}"], writes=[dk], key=(f"o_xp{tt}", "st"))
    P.pop()


def build_program(nlayers=DEPTH):
    P = Prog()
    g = declare(P)
    k = load_consts(P, g)
    for l in range(nlayers):
        src = "in" if l == 0 else "S2"
        last = l == nlayers - 1
        phase_ada(P, g, k, l)
        phase_precast(P, g, l)
        phase_inproj(P, g, k, l, src)
        phase_attn(P, g, k, l)
        phase_s5(P, g, k, l)
        phase_s5_glu(P, g, k, l)
        phase_conv(P, g, k, l)
        P.push()
        hT = P.sbuf("hT2", [128, KT, T], BF16)
        phase_merge(P, g, k, l, src, "S1", hT)
        phase_ffn_up(P, g, k, l, hT)
        P.pop()
        phase_ffn_down(P, g, k, l, "S1", "out" if last else "S2")
    P.barrier()
    P.close()
    return P


def kernel(**inputs):
    x = np.asarray(inputs["x"], np.float32)
    ctx = np.asarray(inputs["ctx"], np.float32)
    c = np.asarray(inputs["c"], np.float32)
    c_ctx = np.asarray(inputs["c_ctx"], np.float32)
    nb = x.shape[0]
    P = build_program()
    cst = host_consts()
    wts = {n: np.ascontiguousarray(np.asarray(inputs[n], np.float32)) for n in WNAMES}
    in_maps = []
    for b in range(nb):
        m = {"x": np.ascontiguousarray(x[b]), "ctx": np.ascontiguousarray(ctx[b]),
             "cvec": make_cvec(c[b], c_ctx), "cst": cst}
        m.update(wts)
        in_maps.append(m)
    res = run_bass_kernel_spmd(P.nc, in_maps, core_ids=list(range(nb)))
    return np.stack([np.asarray(r["out"], np.float32) for r in res.results], axis=0)
```
